# Optimizing a Trainium2 kernel written in Bass

```python
import math
import jax, jax.numpy as jnp
from jax import lax
import numpy as np

D_MODEL = 1024
BATCH = 2
SEQ = 8192
DEPTH = 2

HEAD_DIM = 64
GRID_W = 64
RMS_EPS = 1e-6
NEG = -1e30
A_PATTERNS = ((128, 1), (512, 4), (2048, 16))
A_GROUPS = 3
A_HEADS = 4
A_W = A_GROUPS * A_HEADS * HEAD_DIM
A_OUT = A_HEADS * HEAD_DIM
B_HEADS = 8
B_W = B_HEADS * HEAD_DIM
B_WIN_ROWS = 8
B_WIN_COLS = 16
B_QCOLS = 16
B_KCOLS = 32
C_Q_HEADS = 8
C_KV_HEADS = 2
C_QW = C_Q_HEADS * HEAD_DIM
C_KVW = C_KV_HEADS * HEAD_DIM
C_QBLOCK = 128
ROPE_THETA = 10000.0
ROPE_AXIS_DIM = HEAD_DIM // 2
T5_BUCKETS = 32
T5_MAX_DIST = 1024
N_BRANCH = 3
GATE_W = N_BRANCH * D_MODEL
IN_SIZES = (A_W, A_W, A_W, B_W, B_W, B_W, C_QW, C_KVW, C_KVW, GATE_W)
IN_W = sum(IN_SIZES)
D_FF = math.ceil(8 * D_MODEL / 3 / 256) * 256

kernel_name = "hybrid_dilated_natten_gqa_gated_encoder"


def rms_norm(x, g):
    xf = x.astype(jnp.float32)
    y = xf * lax.rsqrt(jnp.mean(xf * xf, axis=-1, keepdims=True) + RMS_EPS)
    return (y * g.astype(jnp.float32)).astype(x.dtype)


def t5_bucket(rel):
    half = T5_BUCKETS // 2
    max_exact = half // 2
    ret = jnp.where(rel > 0, half, 0)
    n = jnp.abs(rel)
    nf = jnp.maximum(n, 1).astype(jnp.float32)
    large = max_exact + (jnp.log(nf / max_exact) / math.log(T5_MAX_DIST / max_exact)
                         * (half - max_exact)).astype(jnp.int32)
    large = jnp.minimum(large, half - 1)
    return ret + jnp.where(n < max_exact, n, large)


def t5_dilated_bias(table_g, window, rate):
    R = window // (2 * rate)
    i = jnp.arange(R)[:, None]
    j = jnp.arange(3 * R)[None, :]
    rel = (j - R - i) * rate
    return jnp.transpose(table_g[t5_bucket(rel)], (2, 0, 1)).astype(jnp.float32)


def dilated_window_attn(q, k, v, bias, window, rate):
    B, S, H, hd = q.shape
    R = window // (2 * rate)
    L = S // rate
    nb = -(-L // R)
    Lp = nb * R

    def to_sub(t):
        t = t.reshape(B, L, rate, H, hd).transpose(0, 2, 1, 3, 4)
        return jnp.pad(t, ((0, 0), (0, 0), (0, Lp - L), (0, 0), (0, 0)))

    def key_blocks(t):
        tp = jnp.pad(to_sub(t), ((0, 0), (0, 0), (R, R), (0, 0), (0, 0)))
        tp = tp.reshape(B, rate, nb + 2, R, H, hd)
        return jnp.concatenate([tp[:, :, :-2], tp[:, :, 1:-1], tp[:, :, 2:]], axis=3)

    qs = to_sub(q).reshape(B, rate, nb, R, H, hd)
    kb, vb = key_blocks(k), key_blocks(v)
    s = jnp.einsum('brnqhd,brnkhd->brnhqk', qs, kb,
                   preferred_element_type=jnp.float32) * (hd ** -0.5) + bias
    i = jnp.arange(R)[:, None]
    j = jnp.arange(3 * R)[None, :]
    in_win = jnp.abs(j - R - i) <= R
    kpos = jnp.arange(nb)[:, None] * R + jnp.arange(3 * R)[None, :] - R
    valid = (kpos >= 0) & (kpos < L)
    mask = in_win[None, :, :] & valid[:, None, :]
    s = jnp.where(mask[:, None], s, NEG)
    m = jnp.max(s, axis=-1, keepdims=True)
    e = jnp.exp(s - m)
    den = jnp.sum(e, axis=-1, keepdims=True)
    p = (e / den).astype(v.dtype)
    lse = (m + jnp.log(den))[..., 0]
    o = jnp.einsum('brnhqk,brnkhd->brnqhd', p, vb)
    o = o.reshape(B, rate, Lp, H, hd)[:, :, :L].transpose(0, 2, 1, 3, 4).reshape(B, S, H, hd)
    lse = lse.transpose(0, 1, 2, 4, 3).reshape(B, rate, Lp, H)[:, :, :L]
    lse = lse.transpose(0, 2, 1, 3).reshape(B, S, H)
    return o, lse


def neighbourhood_attn(q, k, v, rpb):
    B, S, H, hd = q.shape
    rows = S // GRID_W
    kh = min(B_WIN_ROWS, rows)
    qg = q.reshape(B, rows, GRID_W, H, hd)
    kg = k.reshape(B, rows, GRID_W, H, hd)
    vg = v.reshape(B, rows, GRID_W, H, hd)
    n_cb = GRID_W // B_QCOLS
    mb = jnp.arange(n_cb)
    cs = jnp.clip(mb * B_QCOLS - B_WIN_COLS // 2, 0, GRID_W - B_KCOLS)
    kcol = cs[:, None] + jnp.arange(B_KCOLS)[None, :]
    qcol = mb[:, None] * B_QCOLS + jnp.arange(B_QCOLS)[None, :]
    c0 = jnp.clip(qcol - B_WIN_COLS // 2, 0, GRID_W - B_WIN_COLS)
    col_ok = (kcol[:, None, :] >= c0[..., None]) & (kcol[:, None, :] < c0[..., None] + B_WIN_COLS)
    dc_idx = jnp.clip(kcol[:, None, :] - qcol[..., None] + B_WIN_COLS - 1, 0, 2 * B_WIN_COLS - 2)
    scale = hd ** -0.5

    def one_row(i):
        rs = jnp.clip(i - kh // 2, 0, rows - kh)
        kr = lax.dynamic_slice_in_dim(kg, rs, kh, axis=1)[:, :, kcol]
        vr = lax.dynamic_slice_in_dim(vg, rs, kh, axis=1)[:, :, kcol]
        qr = lax.dynamic_index_in_dim(qg, i, axis=1, keepdims=False).reshape(B, n_cb, B_QCOLS, H, hd)
        s = jnp.einsum('bmqhd,bamchd->bhmqac', qr, kr, preferred_element_type=jnp.float32) * scale
        dr_idx = rs + jnp.arange(kh) - i + B_WIN_ROWS - 1
        bias = rpb[:, dr_idx[None, None, :, None], dc_idx[:, :, None, :]]
        s = jnp.where(col_ok[:, :, None, :], s + bias.astype(jnp.float32), NEG)
        p = jax.nn.softmax(s.reshape(B, H, n_cb, B_QCOLS, kh * B_KCOLS), axis=-1)
        p = p.reshape(B, H, n_cb, B_QCOLS, kh, B_KCOLS).astype(v.dtype)
        o = jnp.einsum('bhmqac,bamchd->bmqhd', p, vr)
        return o.reshape(B, GRID_W, H, hd)

    o = lax.map(one_row, jnp.arange(rows))
    return o.transpose(1, 0, 2, 3, 4).reshape(B, S, H, hd)


def axial_rope_tables(S):
    t = jnp.arange(S)
    inv = ROPE_THETA ** (-jnp.arange(0, ROPE_AXIS_DIM, 2, dtype=jnp.float32) / ROPE_AXIS_DIM)
    ang_r = (t // GRID_W).astype(jnp.float32)[:, None] * inv[None, :]
    ang_c = (t % GRID_W).astype(jnp.float32)[:, None] * inv[None, :]
    return jnp.cos(ang_r), jnp.sin(ang_r), jnp.cos(ang_c), jnp.sin(ang_c)


def rotate(x, cos, sin):
    x1, x2 = jnp.split(x, 2, axis=-1)
    c = cos[None, :, None, :]
    s = sin[None, :, None, :]
    return jnp.concatenate([x1 * c - x2 * s, x1 * s + x2 * c], axis=-1)


def apply_axial_rope(x, tabs):
    cos_r, sin_r, cos_c, sin_c = tabs
    xf = x.astype(jnp.float32)
    out = jnp.concatenate([rotate(xf[..., :ROPE_AXIS_DIM], cos_r, sin_r),
                           rotate(xf[..., ROPE_AXIS_DIM:], cos_c, sin_c)], axis=-1)
    return out.astype(x.dtype)


def gqa_blocked(q, k, v):
    B, S, Hq, hd = q.shape
    Hkv = k.shape[2]
    G = Hq // Hkv
    nqb = S // C_QBLOCK
    qb = q.reshape(B, nqb, C_QBLOCK, Hkv, G, hd).transpose(1, 0, 2, 3, 4, 5)
    scale = hd ** -0.5

    def blk(qi):
        s = jnp.einsum('bqkgd,bskd->bkgqs', qi, k, preferred_element_type=jnp.float32) * scale
        p = jax.nn.softmax(s, axis=-1).astype(v.dtype)
        return jnp.einsum('bkgqs,bskd->bqkgd', p, v)

    o = lax.map(blk, qb)
    return o.transpose(1, 0, 2, 3, 4, 5).reshape(B, S, Hq, hd)


def hybrid_layer(x, t5_biases, rope_tabs, g1, w_in, qk_g, rpb, p_a, p_b, p_c, w_o, g2, w_up, w_down):
    B, S, D = x.shape
    h = rms_norm(x, g1)
    z = h @ w_in
    cuts = np.cumsum(IN_SIZES)[:-1].tolist()
    qa, ka, va, qb, kb, vb, qc, kc, vc, zg = jnp.split(z, cuts, axis=-1)
    qa = rms_norm(qa.reshape(B, S, A_GROUPS, A_HEADS, HEAD_DIM), qk_g[0])
    ka = rms_norm(ka.reshape(B, S, A_GROUPS, A_HEADS, HEAD_DIM), qk_g[1])
    va = va.reshape(B, S, A_GROUPS, A_HEADS, HEAD_DIM)
    outs, lses = [], []
    for g, (win, rate) in enumerate(A_PATTERNS):
        o_g, l_g = dilated_window_attn(qa[:, :, g], ka[:, :, g], va[:, :, g], t5_biases[g], win, rate)
        outs.append(o_g)
        lses.append(l_g)
    wts = jax.nn.softmax(jnp.stack(lses, axis=2), axis=2)
    o_a = jnp.sum(wts[..., None].astype(x.dtype) * jnp.stack(outs, axis=2), axis=2).reshape(B, S, A_OUT)
    qb = rms_norm(qb.reshape(B, S, B_HEADS, HEAD_DIM), qk_g[2])
    kb = rms_norm(kb.reshape(B, S, B_HEADS, HEAD_DIM), qk_g[3])
    o_b = neighbourhood_attn(qb, kb, vb.reshape(B, S, B_HEADS, HEAD_DIM), rpb).reshape(B, S, B_W)
    qc = apply_axial_rope(rms_norm(qc.reshape(B, S, C_Q_HEADS, HEAD_DIM), qk_g[4]), rope_tabs)
    kc = apply_axial_rope(rms_norm(kc.reshape(B, S, C_KV_HEADS, HEAD_DIM), qk_g[5]), rope_tabs)
    o_c = gqa_blocked(qc, kc, vc.reshape(B, S, C_KV_HEADS, HEAD_DIM)).reshape(B, S, C_QW)
    gates = jax.nn.sigmoid(zg.reshape(B, S, N_BRANCH, D))
    merged = gates[:, :, 0] * (o_a @ p_a) + gates[:, :, 1] * (o_b @ p_b) + gates[:, :, 2] * (o_c @ p_c)
    x = x + merged @ w_o
    u = rms_norm(x, g2) @ w_up
    a, b = jnp.split(u, 2, axis=-1)
    return x + (jax.nn.silu(a) * b) @ w_down


def setup_inputs(seed: int = 0) -> dict:
    key = jax.random.key(seed)
    ks = jax.random.split(key, 14)
    f32 = jnp.float32
    nrm = lambda k, shape, s: jax.random.normal(k, shape, f32) * s
    return {
        "x": nrm(ks[0], (BATCH, SEQ, D_MODEL), 1.0),
        "rel_bias_table": nrm(ks[1], (T5_BUCKETS, A_GROUPS * A_HEADS), 0.3),
        "norm1": 1.0 + nrm(ks[2], (DEPTH, D_MODEL), 0.05),
        "w_in": nrm(ks[3], (DEPTH, D_MODEL, IN_W), D_MODEL ** -0.5),
        "qk_gain": 1.0 + nrm(ks[4], (DEPTH, 6, HEAD_DIM), 0.05),
        "nat_rpb": nrm(ks[5], (DEPTH, B_HEADS, 2 * B_WIN_ROWS - 1, 2 * B_WIN_COLS - 1), 0.3),
        "w_br_a": nrm(ks[6], (DEPTH, A_OUT, D_MODEL), A_OUT ** -0.5),
        "w_br_b": nrm(ks[7], (DEPTH, B_W, D_MODEL), B_W ** -0.5),
        "w_br_c": nrm(ks[8], (DEPTH, C_QW, D_MODEL), C_QW ** -0.5),
        "w_o": nrm(ks[9], (DEPTH, D_MODEL, D_MODEL), D_MODEL ** -0.5),
        "norm2": 1.0 + nrm(ks[10], (DEPTH, D_MODEL), 0.05),
        "w_up": nrm(ks[11], (DEPTH, D_MODEL, 2 * D_FF), D_MODEL ** -0.5),
        "w_down": nrm(ks[12], (DEPTH, D_FF, D_MODEL), D_FF ** -0.5),
    }


def reference(x, rel_bias_table, norm1, w_in, qk_gain, nat_rpb, w_br_a, w_br_b, w_br_c, w_o,
              norm2, w_up, w_down):
    S = x.shape[1]
    t5_biases = [t5_dilated_bias(rel_bias_table[:, g * A_HEADS:(g + 1) * A_HEADS], win, rate)
                 for g, (win, rate) in enumerate(A_PATTERNS)]
    rope_tabs = axial_rope_tables(S)
    for l in range(DEPTH):
        x = hybrid_layer(x, t5_biases, rope_tabs, norm1[l], w_in[l], qk_gain[l], nat_rpb[l],
                         w_br_a[l], w_br_b[l], w_br_c[l], w_o[l], norm2[l], w_up[l], w_down[l])
    return x
```

```python
import numpy as np
import ml_dtypes
from contextlib import ExitStack
import concourse.bass as bass
import concourse.mybir as mybir
from concourse.bass_utils import run_bass_kernel_spmd

F32 = mybir.dt.float32
BF16 = mybir.dt.bfloat16
AF = mybir.ActivationFunctionType
ALU = mybir.AluOpType
NPBF = ml_dtypes.bfloat16

NCORES = 8
SEQ = 8192
DM = 1024
NT = 2048
EPS = 1e-6
NEG = -1e30
A_RATES = (1, 4, 16)
NDS = 12


class Prog:
    COMP = ("pe", "act", "dve")
    DMAQ = ("sp", "pool")

    def __init__(self):
        self.nc = bass.Bass("TRN2", target_bir_lowering=False)
        self.es = ExitStack()
        self.ops = {e: [] for e in self.COMP + self.DMAQ}
        self.ncomp = {e: 0 for e in self.COMP}
        self.ndma = {q: 0 for q in self.DMAQ}
        self.lastw = {}
        self.rd = {}
        self.waited = {}
        self.semkeys = set()

    def dram(self, name, shape, dt, out=False):
        kind = "ExternalOutput" if out else "ExternalInput"
        return self.nc.dram_tensor(name, list(shape), dt, kind=kind).ap()

    def sb(self, name, shape, dt):
        return self.es.enter_context(self.nc.sbuf_tensor(name, list(shape), dt))

    def ps(self, name, shape=(128, 512), dt=F32):
        return self.es.enter_context(self.nc.psum_tensor(name, list(shape), dt))

    def op(self, eng, fn, reads=(), writes=()):
        is_dma = eng in self.DMAQ
        deps = []
        for k in reads:
            t = self.lastw.get(k)
            if t is not None:
                deps.append((t, "raw"))
        for k in writes:
            t = self.lastw.get(k)
            if t is not None:
                deps.append((t, "waw"))
            for t in self.rd.get(k, ()):
                deps.append((t, "war"))
        need = {}
        for (semkey, val, teng, tdma), kind in deps:
            if (not tdma) and (not is_dma) and teng == eng and kind != "raw":
                continue
            need[semkey] = max(need.get(semkey, 0), val)
        if is_dma:
            n = self.ndma[eng]
            self.ndma[eng] += 1
            j = n % NDS
            semkey = ("d", eng, j)
            val = 16 * (n // NDS + 1)
            if n >= NDS:
                need[semkey] = max(need.get(semkey, 0), val - 16)
            inc = 16
        else:
            self.ncomp[eng] += 1
            semkey = ("c", eng)
            val = self.ncomp[eng]
            inc = 1
        waits = []
        for sk, v in need.items():
            if self.waited.get((eng, sk), 0) >= v:
                continue
            self.waited[(eng, sk)] = v
            waits.append((sk, v))
            self.semkeys.add(sk)
        self.semkeys.add(semkey)
        self.ops[eng].append((waits, fn, semkey, inc))
        tok = (semkey, val, eng, is_dma)
        for k in writes:
            self.lastw[k] = tok
            self.rd[k] = []
        for k in reads:
            lst = self.rd.setdefault(k, [])
            if not is_dma:
                lst[:] = [t for t in lst if not (t[2] == eng and not t[3])]
            lst.append(tok)
        return tok

    def emit(self):
        nc = self.nc
        sems = {}
        for i, sk in enumerate(sorted(self.semkeys, key=str)):
            sems[sk] = self.es.enter_context(nc.semaphore("s%d" % i))
        ops = self.ops
        ndma = self.ndma

        def mk(eng):
            def body(e):
                for waits, fn, semkey, inc in ops[eng]:
                    for sk, v in waits:
                        e.wait_ge(sems[sk], v)
                    fn(e).then_inc(sems[semkey], inc)
                if eng in ndma:
                    n = ndma[eng]
                    for j in range(min(NDS, n)):
                        cnt = (n - j + NDS - 1) // NDS
                        e.wait_ge(sems[("d", eng, j)], 16 * cnt)
            return body

        with nc.Block() as block:
            block.tensor(mk("pe"))
            block.scalar(mk("act"))
            block.vector(mk("dve"))
            block.gpsimd(mk("pool"))
            block.sync(mk("sp"))
        self.es.close()
        return nc


def mm(out, lhsT, rhs, start, stop):
    return lambda e: e.matmul(out, lhsT=lhsT, rhs=rhs, start=start, stop=stop)


def dma(out, in_):
    return lambda e: e.dma_start(out=out, in_=in_)


def act(out, in_, func, scale=None, bias=None):
    kw = {}
    if scale is not None:
        kw["scale"] = scale
    if bias is not None:
        kw["bias"] = bias
    return lambda e: e.activation(out=out, in_=in_, func=func, **kw)


def stt(out, in0, scalar, in1, op0, op1):
    return lambda e: e.scalar_tensor_tensor(out=out, in0=in0, scalar=scalar, in1=in1, op0=op0, op1=op1)


def tt(out, in0, in1, op):
    return lambda e: e.tensor_tensor(out=out, in0=in0, in1=in1, op=op)


def tcopy(out, in_):
    return lambda e: e.tensor_copy(out=out, in_=in_)


def acopy(out, in_):
    return lambda e: e.copy(out=out, in_=in_)


def recip(out, in_):
    return lambda e: e.reciprocal(out=out, in_=in_)


def tscal(out, in0, s1, s2, op0, op1):
    return lambda e: e.tensor_scalar(out=out, in0=in0, scalar1=s1, scalar2=s2, op0=op0, op1=op1)


def emit_rstd(P, ss_ps, sskey, tmp, tmpkey, inv_n):
    P.op("dve", tscal(tmp, ss_ps, inv_n, EPS, ALU.mult, ALU.add), reads=[sskey], writes=[tmpkey])
    P.op("dve", recip(tmp, tmp), reads=[tmpkey], writes=[tmpkey])
    P.op("act", act(tmp, tmp, AF.Sqrt), reads=[tmpkey], writes=[tmpkey])


def emit_xnorm(P, xT, xkey, hT, hkey, gcol, ones, sq, ss_ps, rstd, ntc):
    for tc in range(ntc):
        ts = slice(tc * 512, (tc + 1) * 512)
        for k in range(8):
            s = k % 2
            P.op("act", act(sq[s][:], xT[:, k, ts], AF.Square), reads=[(xkey, tc)], writes=[("sq", s)])
            P.op("pe", mm(ss_ps, ones, sq[s][:], k == 0, k == 7), reads=[("sq", s), "cst"], writes=["ss_ps"])
        emit_rstd(P, ss_ps, "ss_ps", rstd, "rstd", 1.0 / DM)
        for k in range(8):
            P.op("dve", stt(hT[:, k, ts], xT[:, k, ts], gcol[:, k:k + 1], rstd, ALU.mult, ALU.mult),
                 reads=[(xkey, tc), "rstd", "gcol"], writes=[(hkey, tc)])


NQK = 25
ROPE_CH = (20, 21, 22, 23, 24)


def build_L1():
    P = Prog()
    xT_d = P.dram("xT", [DM, NT], F32)
    g1_d = P.dram("g1c", [128, 8], F32)
    wqk_d = P.dram("wqk", [NQK, 128, 8, 128], F32)
    wv_d = P.dram("wv", [3, 128, 8, 512], F32)
    gq_d = P.dram("gq", [128, NQK], F32)
    cos_d = P.dram("cosT", [128, NT], F32)
    sin_d = P.dram("sinT", [128, NT], F32)
    cst_d = P.dram("cst", [3, 128, 128], F32)
    qk_o = P.dram("qkT", [NQK * 128, NT], BF16, out=True)
    v_o = P.dram("v", [NT, 1536], BF16, out=True)

    xT = P.sb("xT_sb", [128, 8, NT], F32)
    hT = P.sb("hT_sb", [128, 8, NT], BF16)
    g1 = P.sb("g1_sb", [128, 8], F32)
    gq = P.sb("gq_sb", [128, NQK], F32)
    cosT = P.sb("cos_sb", [128, NT], F32)
    sinT = P.sb("sin_sb", [128, NT], F32)
    cst = P.sb("cst_sb", [128, 3, 128], BF16)
    wv = P.sb("wv_sb", [128, 3, 8, 512], BF16)
    NW = 3
    w = [P.sb("w%d" % i, [128, 8, 128], BF16) for i in range(NW)]
    sq = [P.sb("sq%d" % i, [128, 512], BF16) for i in range(2)]
    rstd = P.sb("rstd", [128, 512], F32)
    r32 = [P.sb("r32_%d" % i, [128, 512], F32) for i in range(2)]
    qnb = [P.sb("qnb%d" % i, [128, 512], BF16) for i in range(2)]
    t1 = [P.sb("t1_%d" % i, [128, 512], F32) for i in range(2)]
    t2 = [P.sb("t2_%d" % i, [128, 512], F32) for i in range(2)]
    ob = [P.sb("ob%d" % i, [128, NT], BF16) for i in range(2)]
    vb = [P.sb("vb%d" % i, [128, 1536], BF16) for i in range(2)]
    ss_ps = P.ps("ss_ps")
    z_ps = [P.ps("z_ps%d" % i) for i in range(2)]
    st_ps = [P.ps("st_ps%d" % i) for i in range(2)]
    pm_ps = P.ps("pm_ps")
    v_ps = [P.ps("v_ps%d" % i) for i in range(2)]
    ones, blk, perm = cst[:, 0, :], cst[:, 1, :], cst[:, 2, :]

    P.op("pool", dma(cst[:], cst_d.rearrange("a p c -> p a c")), writes=["cst"])
    P.op("sp", dma(g1[:], g1_d[:, :]), writes=["gcol"])
    P.op("sp", dma(gq[:], gq_d[:, :]), writes=["gq"])
    xv = xT_d.rearrange("(k p) t -> p k t", p=128)
    for tc in range(4):
        P.op("sp", dma(xT[:, :, tc * 512:(tc + 1) * 512], xv[:, :, tc * 512:(tc + 1) * 512]), writes=[("x", tc)])
    P.op("sp", dma(cosT[:], cos_d[:, :]), writes=["cos"])
    P.op("sp", dma(sinT[:], sin_d[:, :]), writes=["sin"])
    for g in range(3):
        P.op("pool", dma(wv[:, g], wv_d[g]), writes=[("wv", g)])

    emit_xnorm(P, xT, "x", hT, "h", g1, ones, sq, ss_ps[:], rstd[:], 4)

    units = [(cc, tc) for cc in range(NQK) for tc in range(4)]

    def stageA(u):
        cc, tc = units[u]
        ws = cc % NW
        if tc == 0:
            P.op("pool", dma(w[ws][:], wqk_d[cc]), writes=[("w", ws)])
        zb = u % 2
        for k in range(8):
            P.op("pe", mm(z_ps[zb][:], w[ws][:, k, :], hT[:, k, tc * 512:(tc + 1) * 512], k == 0, k == 7),
                 reads=[("w", ws), ("h", tc)], writes=[("z", zb)])

    def stageB(u):
        cc, tc = units[u]
        zb = u % 2
        ts = slice(tc * 512, (tc + 1) * 512)
        osl = cc % 2
        P.op("act", act(sq[zb][:], z_ps[zb][:], AF.Square), reads=[("z", zb)], writes=[("sq", zb)])
        P.op("pe", mm(st_ps[zb][:], blk, sq[zb][:], True, True), reads=[("sq", zb), "cst"], writes=[("st", zb)])
        emit_rstd(P, st_ps[zb][:], ("st", zb), r32[zb][:], ("r32", zb), 1.0 / 64)
        rope = cc in ROPE_CH
        dst = qnb[zb][:] if rope else ob[osl][:, ts]
        dkey = ("qnb", zb) if rope else ("ob", osl)
        P.op("dve", stt(dst, z_ps[zb][:], gq[:, cc:cc + 1], r32[zb][:], ALU.mult, ALU.mult),
             reads=[("z", zb), ("r32", zb), "gq"], writes=[dkey])
        if rope:
            P.op("pe", mm(pm_ps[:], perm, qnb[zb][:], True, True), reads=[("qnb", zb), "cst"], writes=["pm"])
            P.op("dve", tt(t1[zb][:], qnb[zb][:], cosT[:, ts], ALU.mult),
                 reads=[("qnb", zb), "cos"], writes=[("t1", zb)])
            P.op("dve", tt(t2[zb][:], pm_ps[:], sinT[:, ts], ALU.mult),
                 reads=["pm", "sin"], writes=[("t2", zb)])
            P.op("dve", tt(ob[osl][:, ts], t1[zb][:], t2[zb][:], ALU.add),
                 reads=[("t1", zb), ("t2", zb)], writes=[("ob", osl)])
        if tc == 3:
            P.op("sp", dma(qk_o[cc * 128:(cc + 1) * 128, :], ob[osl][:]), reads=[("ob", osl)])

    n = len(units)
    for i in range(n + 1):
        if i < n:
            stageA(i)
        if i >= 1:
            stageB(i - 1)

    for tk in range(16):
        vs = tk % 2
        for g in range(3):
            pb = (tk * 3 + g) % 2
            for k in range(8):
                P.op("pe", mm(v_ps[pb][:], hT[:, k, tk * 128:(tk + 1) * 128], wv[:, g, k, :], k == 0, k == 7),
                     reads=[("h", tk // 4), ("wv", g)], writes=[("vps", pb)])
            if g % 2 == 0:
                P.op("act", acopy(vb[vs][:, g * 512:(g + 1) * 512], v_ps[pb][:]),
                     reads=[("vps", pb)], writes=[("vb", vs)])
            else:
                P.op("dve", tcopy(vb[vs][:, g * 512:(g + 1) * 512], v_ps[pb][:]),
                     reads=[("vps", pb)], writes=[("vb", vs)])
        P.op("sp", dma(v_o[tk * 128:(tk + 1) * 128, :], vb[vs][:]), reads=[("vb", vs)])
    return P.emit()


NCLS = 5


def b_tiles(n):
    if n <= 1:
        return [0, 1, 2, 3], 1 + n
    if n >= 62:
        return [60, 61, 62, 63], 3 + (n - 62)
    return [n - 2, n - 1, n, n + 1, n + 2], 0


def emit_norm_out(P, src, srckey, dst_sb, dstkey, sel, den_ps, rden, nparts):
    P.op("pe", mm(den_ps[0:64, :], sel, src[0:65, :], True, True), reads=[srckey, "sel"], writes=["den"])
    P.op("dve", recip(rden[0:64, :], den_ps[0:64, :]), reads=["den"], writes=["rden"])
    P.op("dve", tt(dst_sb, src[0:64, :], rden[0:64, :], ALU.mult), reads=[srckey, "rden"], writes=[dstkey])


def build_ATT():
    P = Prog()
    aq_d = P.dram("aq", [3, 64, SEQ], BF16)
    ak_d = P.dram("ak", [3, 64, SEQ + 2048], BF16)
    av_d = P.dram("av", [3, 128, 80, 65], BF16)
    ab_d = P.dram("abias", [3, 128, 256], F32)
    bq_d = P.dram("bq", [128, SEQ], BF16)
    bk_d = P.dram("bk", [128, SEQ], BF16)
    bv_d = P.dram("bv", [2, 128, 64, 65], BF16)
    bb_d = P.dram("bbias", [2, NCLS, 128, 640], F32)
    cq_d = P.dram("cq", [128, SEQ], BF16)
    ck_d = P.dram("ck", [128, SEQ], BF16)
    cv_d = P.dram("cv", [128, 64, 65], BF16)
    sel_d = P.dram("sel", [65, 64], F32)
    oa_o = P.dram("oa", [64, SEQ], BF16, out=True)
    ob_o = P.dram("ob", [128, SEQ], BF16, out=True)
    oc_o = P.dram("oc", [128, SEQ], BF16, out=True)

    q_sb = P.sb("q_sb", [128, SEQ], BF16)
    k_sb = P.sb("k_sb", [128, SEQ + 2048], BF16)
    v_sb = P.sb("v_sb", [128, 2, 80, 65], BF16)
    abias = P.sb("abias_sb", [128, 3, 256], F32)
    bbias = P.sb("bbias_sb", [128, 2, NCLS, 640], F32)
    sel = P.sb("sel_sb", [65, 64], F32)
    oacc = P.sb("oacc", [65, SEQ], F32)
    sc = [P.sb("sc%d" % i, [128, 640], F32) for i in range(2)]
    pT = [P.sb("pT%d" % i, [128, 640], BF16) for i in range(4)]
    stg = [P.sb("stg%d" % i, [65, 512], F32) for i in range(2)]
    rden = P.sb("rden", [64, 512], F32)
    osb = [P.sb("osb%d" % i, [128, 512], BF16) for i in range(2)]
    s_ps = [P.ps("s_ps%d" % i) for i in range(4)]
    o_ps = [P.ps("o_ps%d" % i) for i in range(2)]
    den_ps = P.ps("den_ps")

    P.op("sp", dma(sel[:], sel_d[:, :]), writes=["sel"])
    P.op("sp", dma(abias[:], ab_d.rearrange("g p c -> p g c")), writes=["abias"])
    for hi in range(2):
        P.op("sp", dma(bbias[:, hi], bb_d[hi].rearrange("c p x -> p c x")), writes=["bbias"])

    for g, rate in enumerate(A_RATES):
        L = SEQ // rate
        nqb = L // 128
        P.op("sp", dma(q_sb[0:64, :], aq_d[g]), writes=["q"])
        P.op("sp", dma(k_sb[0:64, :], ak_d[g]), writes=["k"])
        P.op("sp", dma(v_sb[:, 0], av_d[g]), writes=["v"])
        qv = q_sb[0:64, :].rearrange("p (l r) -> p r l", r=rate)
        kv = k_sb[0:64, :].rearrange("p (l r) -> p r l", r=rate)
        ov = oacc[:, :].rearrange("p (l r) -> p r l", r=rate)
        koff = 1024 // rate
        units = [(res, qb) for res in range(rate) for qb in range(nqb)]

        def stA(u, g=g, qv=qv, kv=kv, koff=koff, units=units):
            res, qb = units[u]
            sb_ = u % 2
            qs = qv[:, res, 128 * qb:128 * qb + 128]
            l0 = koff + 128 * qb - 64
            P.op("pe", mm(s_ps[sb_][:, 0:128], kv[:, res, l0:l0 + 128], qs, True, True),
                 reads=["q", "k"], writes=[("s", sb_)])
            P.op("pe", mm(s_ps[sb_][:, 128:256], kv[:, res, l0 + 128:l0 + 256], qs, True, True),
                 reads=["q", "k"], writes=[("s", sb_)])

        def stB(u, g=g, ov=ov, units=units, nqb=nqb):
            res, qb = units[u]
            sb_ = u % 2
            P.op("dve", stt(sc[sb_][:, 0:256], s_ps[sb_][:, 0:256], 0.125, abias[:, g, :], ALU.mult, ALU.add),
                 reads=[("s", sb_), "abias"], writes=[("sc", sb_)])
            P.op("act", act(pT[sb_][:, 0:256], sc[sb_][:, 0:256], AF.Exp), reads=[("sc", sb_)], writes=[("pT", sb_)])
            tA = res * (nqb + 1) + qb
            P.op("pe", mm(o_ps[sb_][0:65, 0:128], v_sb[:, 0, tA, :], pT[sb_][:, 0:128], True, False),
                 reads=[("pT", sb_), "v"], writes=[("o", sb_)])
            P.op("pe", mm(o_ps[sb_][0:65, 0:128], v_sb[:, 0, tA + 1, :], pT[sb_][:, 128:256], False, True),
                 reads=[("pT", sb_), "v"], writes=[("o", sb_)])
            dst = ov[:, res, 128 * qb:128 * qb + 128]
            if g == 0:
                P.op("dve", tcopy(dst, o_ps[sb_][0:65, 0:128]), reads=[("o", sb_)], writes=["oacc"])
            else:
                P.op("dve", tt(dst, o_ps[sb_][0:65, 0:128], dst, ALU.add), reads=[("o", sb_), "oacc"], writes=["oacc"])

        n = len(units)
        for i in range(n + 1):
            if i < n:
                stA(i)
            if i >= 1:
                stB(i - 1)
    for c in range(16):
        cs = slice(c * 512, (c + 1) * 512)
        o_ = c % 2
        emit_norm_out(P, oacc[:, cs], "oacc", osb[o_][0:64, :], ("osb", o_), sel[:], den_ps, rden, 64)
        P.op("sp", dma(oa_o[:, cs], osb[o_][0:64, :]), reads=[("osb", o_)])

    P.op("sp", dma(q_sb[:, :], bq_d[:, :]), writes=["q"])
    P.op("sp", dma(k_sb[:, 0:SEQ], bk_d[:, :]), writes=["k"])
    for hi in range(2):
        P.op("sp", dma(v_sb[:, hi, 0:64, :], bv_d[hi]), writes=["v"])
    unitsB = [(c, j, hi) for c in range(16) for j in range(4) for hi in range(2)]

    def sBA(u):
        c, j, hi = unitsB[u]
        n_ = 4 * c + j
        kts, cls = b_tiles(n_)
        hp = slice(64 * hi, 64 * hi + 64)
        qs = q_sb[hp, 128 * n_:128 * n_ + 128]
        sa, sb2 = s_ps[2 * (u % 2)], s_ps[2 * (u % 2) + 1]
        for idx, kt in enumerate(kts):
            dst = sa[:, idx * 128:(idx + 1) * 128] if idx < 4 else sb2[:, 0:128]
            P.op("pe", mm(dst, k_sb[hp, 128 * kt:128 * kt + 128], qs, True, True),
                 reads=["q", "k"], writes=[("s", 2 * (u % 2) + (0 if idx < 4 else 1))])

    def sBB(u):
        c, j, hi = unitsB[u]
        n_ = 4 * c + j
        kts, cls = b_tiles(n_)
        b2 = u % 2
        sa, sb2 = s_ps[2 * b2], s_ps[2 * b2 + 1]
        P.op("dve", stt(sc[b2][:, 0:512], sa[:, :], 0.125, bbias[:, hi, cls, 0:512], ALU.mult, ALU.add),
             reads=[("s", 2 * b2), "bbias"], writes=[("sc", b2)])
        nk = len(kts)
        if nk == 5:
            P.op("dve", stt(sc[b2][:, 512:640], sb2[:, 0:128], 0.125, bbias[:, hi, cls, 512:640], ALU.mult, ALU.add),
                 reads=[("s", 2 * b2 + 1), "bbias"], writes=[("sc", b2)])
        P.op("act", act(pT[b2][:, 0:nk * 128], sc[b2][:, 0:nk * 128], AF.Exp), reads=[("sc", b2)], writes=[("pT", b2)])
        for idx, kt in enumerate(kts):
            P.op("pe", mm(o_ps[hi][0:65, j * 128:(j + 1) * 128], v_sb[:, hi, kt, :], pT[b2][:, idx * 128:(idx + 1) * 128],
                          idx == 0, idx == nk - 1), reads=[("pT", b2), "v"], writes=[("o", hi)])
        if j == 3:
            st_ = hi
            P.op("act", acopy(stg[st_][:, :], o_ps[hi][0:65, :]), reads=[("o", hi)], writes=[("stg", st_)])
            emit_norm_out(P, stg[st_], ("stg", st_), osb[st_][0:64, :], ("osb", st_), sel[:], den_ps, rden, 64)
            P.op("sp", dma(ob_o[64 * hi:64 * hi + 64, c * 512:(c + 1) * 512], osb[st_][0:64, :]), reads=[("osb", st_)])

    n = len(unitsB)
    for i in range(n + 1):
        if i < n:
            sBA(i)
        if i >= 1:
            sBB(i - 1)

    P.op("sp", dma(q_sb[:, :], cq_d[:, :]), writes=["q"])
    P.op("sp", dma(k_sb[:, 0:SEQ], ck_d[:, :]), writes=["k"])
    P.op("sp", dma(v_sb[:, 0, 0:64, :], cv_d[:, :, :]), writes=["v"])
    unitsC = [(c, kt) for c in range(16) for kt in range(64)]

    def sCA(u):
        c, kt = unitsC[u]
        for hi in range(2):
            hp = slice(64 * hi, 64 * hi + 64)
            sb_ = 2 * (u % 2) + hi
            P.op("pe", mm(s_ps[sb_][:, :], k_sb[hp, 128 * kt:128 * kt + 128], q_sb[hp, c * 512:(c + 1) * 512], True, True),
                 reads=["q", "k"], writes=[("s", sb_)])

    def sCB(u):
        c, kt = unitsC[u]
        for hi in range(2):
            sb_ = 2 * (u % 2) + hi
            P.op("act", act(pT[sb_][:, 0:512], s_ps[sb_][:, :], AF.Exp, scale=0.125),
                 reads=[("s", sb_)], writes=[("pT", sb_)])
            P.op("pe", mm(o_ps[hi][0:65, :], v_sb[:, 0, kt, :], pT[sb_][:, 0:512], kt == 0, kt == 63),
                 reads=[("pT", sb_), "v"], writes=[("o", hi)])
            if kt == 63:
                P.op("dve", tcopy(stg[hi][:, :], o_ps[hi][0:65, :]), reads=[("o", hi)], writes=[("stg", hi)])
                emit_norm_out(P, stg[hi], ("stg", hi), osb[hi][0:64, :], ("osb", hi), sel[:], den_ps, rden, 64)
                P.op("sp", dma(oc_o[64 * hi:64 * hi + 64, c * 512:(c + 1) * 512], osb[hi][0:64, :]),
                     reads=[("osb", hi)])

    n = len(unitsC)
    for i in range(n + 1):
        if i < n:
            sCA(i)
        if i >= 1:
            sCB(i - 1)
    return P.emit()


TB = 1024
OCH = ((0, 2), (2, 6), (6, 10))


def build_L2():
    P = Prog()
    xT_d = P.dram("xT", [DM, NT], F32)
    oT_d = P.dram("oT", [1280, NT], BF16)
    g1_d = P.dram("g1c", [128, 8], F32)
    g2_d = P.dram("g2c", [128, 8], F32)
    wg_d = P.dram("wg", [8, 128, 3, 8, 128], F32)
    pw_d = P.dram("pw", [8, 128, 10, 128], F32)
    wo_d = P.dram("wo", [8, 128, 8, 128], F32)
    wup_d = P.dram("wup", [22, 128, 2, 8, 128], F32)
    wdn_d = P.dram("wdn", [8, 128, 22, 128], F32)
    one_d = P.dram("ones", [128, 128], F32)
    y_o = P.dram("yT", [DM, NT], F32, out=True)

    xT = P.sb("xT_sb", [128, 8, TB], F32)
    hT = P.sb("hT_sb", [128, 8, TB], BF16)
    oT = P.sb("oT_sb", [128, 10, TB], BF16)
    mT = P.sb("mT_sb", [128, 8, TB], BF16)
    aT = P.sb("aT_sb", [128, 22, TB], BF16)
    g1 = P.sb("g1_sb", [128, 8], F32)
    g2 = P.sb("g2_sb", [128, 8], F32)
    ones = P.sb("ones_sb", [128, 128], BF16)
    wg = [P.sb("wg%d" % i, [128, 3, 8, 128], BF16) for i in range(2)]
    pw = [P.sb("pw%d" % i, [128, 10, 128], BF16) for i in range(2)]
    wo = [P.sb("wo%d" % i, [128, 8, 128], BF16) for i in range(2)]
    wup = [P.sb("wup%d" % i, [128, 2, 8, 128], BF16) for i in range(2)]
    wdn = [P.sb("wdn%d" % i, [128, 22, 128], BF16) for i in range(2)]
    sq = [P.sb("sq%d" % i, [128, 512], BF16) for i in range(2)]
    rstd = P.sb("rstd", [128, 512], F32)
    gs = [P.sb("gs%d" % i, [128, 512], F32) for i in range(2)]
    macc = [P.sb("macc%d" % i, [128, 512], F32) for i in range(2)]
    mtmp = [P.sb("mtmp%d" % i, [128, 512], F32) for i in range(2)]
    ss_ps = P.ps("ss_ps")
    g_ps = [P.ps("g_ps%d" % i) for i in range(2)]
    p_ps = [P.ps("p_ps%d" % i) for i in range(2)]
    y_ps = [P.ps("y_ps%d" % i) for i in range(2)]

    P.op("pool", dma(ones[:], one_d[:, :]), writes=["cst"])
    P.op("sp", dma(g1[:], g1_d[:, :]), writes=["gcol"])
    P.op("sp", dma(g2[:], g2_d[:, :]), writes=["gcol2"])
    xv = xT_d.rearrange("(k p) t -> p k t", p=128)
    ov = oT_d.rearrange("(j p) t -> p j t", p=128)
    yv = y_o.rearrange("(k p) t -> p k t", p=128)
    cnt = {"g": 0, "p": 0, "y": 0, "gs": 0, "m": 0}

    for half in range(NT // TB):
        h0 = half * TB
        for tc in range(2):
            ts = slice(tc * 512, (tc + 1) * 512)
            P.op("sp", dma(xT[:, :, ts], xv[:, :, h0 + tc * 512:h0 + (tc + 1) * 512]), writes=[("x", tc)])
        for tc in range(2):
            ts = slice(tc * 512, (tc + 1) * 512)
            P.op("sp", dma(oT[:, :, ts], ov[:, :, h0 + tc * 512:h0 + (tc + 1) * 512]), writes=[("o", tc)])
        emit_xnorm(P, xT, "x", hT, "h", g1, ones[:], sq, ss_ps[:], rstd[:], 2)

        for dc in range(8):
            ws = dc % 2
            P.op("pool", dma(wg[ws][:], wg_d[dc]), writes=[("wg", ws)])
            P.op("pool", dma(pw[ws][:], pw_d[dc]), writes=[("pw", ws)])
            for tc in range(2):
                ts = slice(tc * 512, (tc + 1) * 512)
                ma = cnt["m"] % 2
                cnt["m"] += 1
                for br in range(3):
                    gb = cnt["g"] % 2
                    cnt["g"] += 1
                    for k in range(8):
                        P.op("pe", mm(g_ps[gb][:], wg[ws][:, br, k, :], hT[:, k, ts], k == 0, k == 7),
                             reads=[("wg", ws), ("h", tc)], writes=[("g_ps", gb)])
                    P.op("act", act(gs[gb][:], g_ps[gb][:], AF.Sigmoid), reads=[("g_ps", gb)], writes=[("gs", gb)])
                    j0, j1 = OCH[br]
                    for j in range(j0, j1):
                        P.op("pe", mm(p_ps[gb][:], pw[ws][:, j, :], oT[:, j, ts], j == j0, j == j1 - 1),
                             reads=[("pw", ws), ("o", tc)], writes=[("p_ps", gb)])
                    if br == 0:
                        P.op("dve", tt(macc[ma][:], p_ps[gb][:], gs[gb][:], ALU.mult),
                             reads=[("p_ps", gb), ("gs", gb)], writes=[("macc", ma)])
                    else:
                        P.op("dve", tt(mtmp[ma][:], p_ps[gb][:], gs[gb][:], ALU.mult),
                             reads=[("p_ps", gb), ("gs", gb)], writes=[("mtmp", ma)])
                        dst = macc[ma][:] if br == 1 else mT[:, dc, ts]
                        dkey = ("macc", ma) if br == 1 else ("m", tc)
                        P.op("dve", tt(dst, macc[ma][:], mtmp[ma][:], ALU.add),
                             reads=[("macc", ma), ("mtmp", ma)], writes=[dkey])

        for oc in range(8):
            ws = oc % 2
            P.op("pool", dma(wo[ws][:], wo_d[oc]), writes=[("wo", ws)])
            for tc in range(2):
                ts = slice(tc * 512, (tc + 1) * 512)
                yb = cnt["y"] % 2
                cnt["y"] += 1
                for k in range(8):
                    P.op("pe", mm(y_ps[yb][:], wo[ws][:, k, :], mT[:, k, ts], k == 0, k == 7),
                         reads=[("wo", ws), ("m", tc)], writes=[("y_ps", yb)])
                P.op("dve", tt(xT[:, oc, ts], y_ps[yb][:], xT[:, oc, ts], ALU.add),
                     reads=[("y_ps", yb), ("x", tc)], writes=[("x", tc)])

        emit_xnorm(P, xT, "x", hT, "h", g2, ones[:], sq, ss_ps[:], rstd[:], 2)
        for fc in range(22):
            ws = fc % 2
            P.op("pool", dma(wup[ws][:], wup_d[fc]), writes=[("wup", ws)])
            for tc in range(2):
                ts = slice(tc * 512, (tc + 1) * 512)
                gb = cnt["g"] % 2
                cnt["g"] += 1
                for k in range(8):
                    P.op("pe", mm(g_ps[gb][:], wup[ws][:, 0, k, :], hT[:, k, ts], k == 0, k == 7),
                         reads=[("wup", ws), ("h", tc)], writes=[("g_ps", gb)])
                for k in range(8):
                    P.op("pe", mm(p_ps[gb][:], wup[ws][:, 1, k, :], hT[:, k, ts], k == 0, k == 7),
                         reads=[("wup", ws), ("h", tc)], writes=[("p_ps", gb)])
                P.op("act", act(gs[gb][:], g_ps[gb][:], AF.Silu), reads=[("g_ps", gb)], writes=[("gs", gb)])
                P.op("dve", tt(aT[:, fc, ts], p_ps[gb][:], gs[gb][:], ALU.mult),
                     reads=[("p_ps", gb), ("gs", gb)], writes=[("a", tc)])
        for oc in range(8):
            ws = oc % 2
            P.op("pool", dma(wdn[ws][:], wdn_d[oc]), writes=[("wdn", ws)])
            for tc in range(2):
                ts = slice(tc * 512, (tc + 1) * 512)
                yb = cnt["y"] % 2
                cnt["y"] += 1
                for f in range(22):
                    P.op("pe", mm(y_ps[yb][:], wdn[ws][:, f, :], aT[:, f, ts], f == 0, f == 21),
                         reads=[("wdn", ws), ("a", tc)], writes=[("y_ps", yb)])
                P.op("dve", tt(xT[:, oc, ts], y_ps[yb][:], xT[:, oc, ts], ALU.add),
                     reads=[("y_ps", yb), ("x", tc)], writes=[("x", tc)])
        for tc in range(2):
            ts = slice(tc * 512, (tc + 1) * 512)
            P.op("sp", dma(yv[:, :, h0 + tc * 512:h0 + (tc + 1) * 512], xT[:, :, ts]), reads=[("x", tc)])
    return P.emit()


QK_COLS = ([128 * j for j in range(6)] + [768 + 128 * j for j in range(6)] + [2304 + 128 * j for j in range(4)]
           + [2816 + 128 * j for j in range(4)] + [3840 + 128 * j for j in range(4)] + [4352])
QK_GAIN = [0] * 6 + [1] * 6 + [2] * 4 + [3] * 4 + [4] * 4 + [5]


def chunk_major(w, cols, width):
    K = w.shape[0]
    wk = w.reshape(K // 128, 128, w.shape[1])
    return np.ascontiguousarray(np.stack([wk[:, :, c0:c0 + width].transpose(1, 0, 2) for c0 in cols], 0))


def t5_bucket_np(rel):
    half, max_exact = 16, 8
    ret = np.where(rel > 0, half, 0)
    n = np.abs(rel)
    nf = np.maximum(n, 1).astype(np.float32)
    large = max_exact + (np.log(nf / np.float32(max_exact)) / np.float32(np.log(1024 / max_exact))
                         * np.float32(half - max_exact)).astype(np.int32)
    large = np.minimum(large, half - 1)
    return ret + np.where(n < max_exact, n, large)


def rope_tables():
    t = np.arange(SEQ)
    inv = (np.float32(10000.0) ** (-np.arange(0, 32, 2, dtype=np.float32) / np.float32(32))).astype(np.float32)
    ang_r = (t // 64).astype(np.float32)[:, None] * inv[None, :]
    ang_c = (t % 64).astype(np.float32)[:, None] * inv[None, :]
    cr, sr, cc_, sc_ = np.cos(ang_r), np.sin(ang_r), np.cos(ang_c), np.sin(ang_c)
    cosT = np.concatenate([cr, cr, cc_, cc_], 1).T
    sinT = np.concatenate([-sr, sr, -sc_, sc_], 1).T
    return (np.ascontiguousarray(np.concatenate([cosT, cosT], 0), dtype=np.float32),
            np.ascontiguousarray(np.concatenate([sinT, sinT], 0), dtype=np.float32))


def consts():
    ones = np.ones((128, 128), np.float32)
    blk = np.zeros((128, 128), np.float32)
    blk[:64, :64] = 1
    blk[64:, 64:] = 1
    perm = np.zeros((128, 128), np.float32)
    for m in range(128):
        d = m % 64
        partner = d + 16 if (d % 32) < 16 else d - 16
        perm[(m // 64) * 64 + partner, m] = 1
    return np.stack([ones, blk, perm], 0)


def a_bias(table, r):
    out = np.empty((3, 128, 256), np.float32)
    i = np.arange(128)[:, None]
    j = np.arange(128)[None, :]
    for g, rate in enumerate(A_RATES):
        for t, off in enumerate((-64, 64)):
            rel = i - j + off
            b = table[t5_bucket_np(rel * rate), g * 4 + r]
            out[g, :, t * 128:(t + 1) * 128] = np.where(np.abs(rel) <= 64, b, NEG)
    return out


def b_bias(rpb, r):
    out = np.full((2, NCLS, 128, 640), NEG, np.float32)
    reps = {0: 2, 1: 0, 2: 1, 3: 62, 4: 63}
    ki = np.arange(128)[:, None]
    qj = np.arange(128)[None, :]
    for cls, n in reps.items():
        kts, c2 = b_tiles(n)
        assert c2 == cls
        qrow = 2 * n + qj // 64
        qcol = qj % 64
        rs = np.clip(qrow - 4, 0, 120)
        c0 = np.clip(qcol - 8, 0, 48)
        for idx, kt in enumerate(kts):
            krow = 2 * kt + ki // 64
            kcol = ki % 64
            ok = (krow >= rs) & (krow < rs + 8) & (kcol >= c0) & (kcol < c0 + 16)
            dr = np.clip(krow - qrow + 7, 0, 14)
            dc = np.clip(kcol - qcol + 15, 0, 30)
            for hi in range(2):
                b = rpb[2 * r + hi][dr, dc]
                out[hi, cls, :, idx * 128:(idx + 1) * 128] = np.where(ok, b, NEG)
    return out


def v_aug_tiles(v):
    n = v.shape[0] // 128
    va = np.concatenate([v, np.ones((v.shape[0], 1), v.dtype)], 1).reshape(n, 128, 65)
    return np.ascontiguousarray(va.transpose(1, 0, 2))


def a_v_tiles(v):
    out = np.zeros((3, 128, 80, 65), v[0].dtype)
    for g, rate in enumerate(A_RATES):
        L = SEQ // rate
        va = np.concatenate([v[g], np.ones((SEQ, 1), v[g].dtype)], 1)
        sub = va.reshape(L, rate, 65).transpose(1, 0, 2)
        pad = np.zeros((rate, L + 128, 65), v[g].dtype)
        pad[:, 64:64 + L] = sub
        tiles = pad.reshape(rate * (L // 128 + 1), 128, 65)
        out[g, :, :tiles.shape[0]] = tiles.transpose(1, 0, 2)
    return out


_NC = {}


def get_nc(name):
    if name not in _NC:
        _NC[name] = {"L1": build_L1, "ATT": build_ATT, "L2": build_L2}[name]()
    return _NC[name]


def run(name, in_maps):
    res = run_bass_kernel_spmd(get_nc(name), in_maps, core_ids=list(range(NCORES)))
    return res.results


def prep_L1(xT_cores, l, p, tabs):
    cosT, sinT = tabs
    w_in = p["w_in"][l]
    wqk = chunk_major(w_in, QK_COLS, 128)
    vcols = np.concatenate([w_in[:, 1536:2304], w_in[:, 3328:3840], w_in[:, 4480:4608],
                            np.zeros((DM, 128), np.float32)], 1)
    wv = chunk_major(vcols, [0, 512, 1024], 512)
    g1c = np.ascontiguousarray(p["norm1"][l].reshape(8, 128).T)
    gq = np.ascontiguousarray(np.stack([np.tile(p["qk_gain"][l][gi], 2) for gi in QK_GAIN], 1))
    cst = consts()
    maps = []
    for c in range(NCORES):
        r = c % 4
        maps.append({"xT": xT_cores[c], "g1c": g1c, "wqk": wqk, "wv": wv, "gq": gq,
                     "cosT": np.ascontiguousarray(cosT[:, r * NT:(r + 1) * NT]),
                     "sinT": np.ascontiguousarray(sinT[:, r * NT:(r + 1) * NT]), "cst": cst})
    return maps


def prep_ATT(l1res, l, p):
    sel = np.zeros((65, 64), np.float32)
    sel[64, :] = 1
    maps = []
    for b in range(2):
        qk = np.concatenate([l1res[4 * b + r]["qkT"] for r in range(4)], 1)
        v = np.concatenate([l1res[4 * b + r]["v"] for r in range(4)], 0)
        for r in range(4):
            m = {}
            ro = 64 * (r % 2)
            m["aq"] = np.ascontiguousarray(np.stack(
                [qk[(2 * g + r // 2) * 128 + ro:(2 * g + r // 2) * 128 + ro + 64] for g in range(3)], 0))
            ak = np.zeros((3, 64, SEQ + 2048), qk.dtype)
            for g in range(3):
                ak[g, :, 1024:1024 + SEQ] = qk[(6 + 2 * g + r // 2) * 128 + ro:(6 + 2 * g + r // 2) * 128 + ro + 64]
            m["ak"] = ak
            m["av"] = a_v_tiles([v[:, g * 256 + r * 64:g * 256 + r * 64 + 64] for g in range(3)])
            m["abias"] = a_bias(p["rel_bias_table"], r)
            m["bq"] = np.ascontiguousarray(qk[(12 + r) * 128:(13 + r) * 128])
            m["bk"] = np.ascontiguousarray(qk[(16 + r) * 128:(17 + r) * 128])
            m["bv"] = np.stack([v_aug_tiles(v[:, 768 + (2 * r + hi) * 64:768 + (2 * r + hi) * 64 + 64])
                                for hi in range(2)], 0)
            m["bbias"] = b_bias(p["nat_rpb"][l], r)
            m["cq"] = np.ascontiguousarray(qk[(20 + r) * 128:(21 + r) * 128])
            kc = qk[24 * 128 + 64 * (r // 2):24 * 128 + 64 * (r // 2) + 64]
            m["ck"] = np.ascontiguousarray(np.concatenate([kc, kc], 0))
            m["cv"] = v_aug_tiles(v[:, 1280 + (r // 2) * 64:1280 + (r // 2) * 64 + 64])
            m["sel"] = sel
            maps.append(m)
    return maps


def prep_L2(xT_cores, attres, l, p):
    w_in = p["w_in"][l]
    wg = np.ascontiguousarray(np.stack(
        [chunk_major(w_in, [4608 + br * 1024 + dc * 128 for dc in range(8)], 128) for br in range(3)], 2))
    pcat = np.concatenate([p["w_br_a"][l], p["w_br_b"][l], p["w_br_c"][l]], 0)
    pw = chunk_major(pcat, [dc * 128 for dc in range(8)], 128)
    wo = chunk_major(p["w_o"][l], [oc * 128 for oc in range(8)], 128)
    w_up = p["w_up"][l]
    wup = np.ascontiguousarray(np.stack(
        [chunk_major(w_up, [ab * 2816 + fc * 128 for fc in range(22)], 128) for ab in range(2)], 2))
    wdn = chunk_major(p["w_down"][l], [oc * 128 for oc in range(8)], 128)
    g1c = np.ascontiguousarray(p["norm1"][l].reshape(8, 128).T)
    g2c = np.ascontiguousarray(p["norm2"][l].reshape(8, 128).T)
    ones = np.ones((128, 128), np.float32)
    maps = []
    for b in range(2):
        oa = np.concatenate([attres[4 * b + r]["oa"] for r in range(4)], 0)
        ob = np.concatenate([attres[4 * b + r]["ob"] for r in range(4)], 0)
        oc = np.concatenate([attres[4 * b + r]["oc"] for r in range(4)], 0)
        oT = np.concatenate([oa, ob, oc], 0)
        for r in range(4):
            maps.append({"xT": xT_cores[4 * b + r], "oT": np.ascontiguousarray(oT[:, r * NT:(r + 1) * NT]),
                         "g1c": g1c, "g2c": g2c, "wg": wg, "pw": pw, "wo": wo, "wup": wup, "wdn": wdn,
                         "ones": ones})
    return maps


def kernel(x, rel_bias_table, norm1, w_in, qk_gain, nat_rpb, w_br_a, w_br_b, w_br_c, w_o, norm2, w_up, w_down):
    p = dict(rel_bias_table=np.asarray(rel_bias_table, np.float32), norm1=np.asarray(norm1, np.float32),
             w_in=np.asarray(w_in, np.float32), qk_gain=np.asarray(qk_gain, np.float32),
             nat_rpb=np.asarray(nat_rpb, np.float32), w_br_a=np.asarray(w_br_a, np.float32),
             w_br_b=np.asarray(w_br_b, np.float32), w_br_c=np.asarray(w_br_c, np.float32),
             w_o=np.asarray(w_o, np.float32), norm2=np.asarray(norm2, np.float32),
             w_up=np.asarray(w_up, np.float32), w_down=np.asarray(w_down, np.float32))
    x = np.asarray(x, np.float32)
    tabs = rope_tables()
    xT = [np.ascontiguousarray(x[c // 4, (c % 4) * NT:(c % 4 + 1) * NT, :].T) for c in range(NCORES)]
    for l in range(2):
        r1 = run("L1", prep_L1(xT, l, p, tabs))
        ra = run("ATT", prep_ATT(r1, l, p))
        r2 = run("L2", prep_L2(xT, ra, l, p))
        xT = [r2[c]["yT"] for c in range(NCORES)]
    out = np.empty((2, SEQ, DM), np.float32)
    for c in range(NCORES):
        out[c // 4, (c % 4) * NT:(c % 4 + 1) * NT, :] = xT[c].T
    return out
```

```python
import numpy as np
import ml_dtypes
from contextlib import ExitStack
import concourse.bass as bass
import concourse.mybir as mybir
from concourse.bass_utils import run_bass_kernel_spmd

F32 = mybir.dt.float32
BF16 = mybir.dt.bfloat16
AF = mybir.ActivationFunctionType
ALU = mybir.AluOpType
NPBF = ml_dtypes.bfloat16

NCORES = 8
SEQ = 8192
DM = 1024
NT = 2048
EPS = 1e-6
NEG = -1e30
A_RATES = (1, 4, 16)
NDS = 12
ARENA = 46592
PADT = 1024


class Prog:
    COMP = ("pe", "act", "dve")
    DMAQ = ("sp", "pool")

    def __init__(self):
        self.nc = bass.Bass("TRN2", target_bir_lowering=False)
        self.es = ExitStack()
        self.ops = {e: [] for e in self.COMP + self.DMAQ}
        self.ncomp = {e: 0 for e in self.COMP}
        self.ndma = {q: 0 for q in self.DMAQ}
        self.lastw = {}
        self.rd = {}
        self.waited = {}
        self.semkeys = set()
        self.big = self.es.enter_context(self.nc.sbuf_tensor("arena", [128, ARENA], F32))
        self.aoff = 0
        self.psall = self.es.enter_context(self.nc.psum_tensor("psum_all", [128, 4096], F32))
        self.bank = [self.psall[:, 512 * i:512 * (i + 1)] for i in range(8)]
        self._own = {}
        self.eps_t = self.es.enter_context(self.nc.sbuf_tensor("eps_c", [128, 1], F32))
        self.eps = self.eps_t[:]

    def arena_reset(self):
        self.aoff = 0

    def _carve(self, shape, words):
        words = (words + 15) // 16 * 16
        assert self.aoff + words <= ARENA, ("arena overflow", self.aoff, words)
        ap = self.big[0:shape[0], self.aoff:self.aoff + words]
        self.aoff += words
        return ap

    @staticmethod
    def _shape(ap, shape):
        if len(shape) == 2:
            return ap
        if len(shape) == 3:
            return ap.rearrange("p (a b) -> p a b", a=shape[1])
        return ap.rearrange("p (a b c) -> p a b c", a=shape[1], b=shape[2])

    def a32(self, shape):
        n = int(np.prod(shape[1:]))
        return self._shape(self._carve(shape, n)[:, 0:n], shape)

    def a16(self, shape):
        n = int(np.prod(shape[1:]))
        return self._shape(self._carve(shape, (n + 1) // 2).bitcast(BF16)[:, 0:n], shape)

    def own(self, e):
        k = id(e)
        if k not in self._own:
            self._own[k] = (e.partition_id() % 4) * NT
        return self._own[k]

    def barrier(self):
        tg = {}
        for en in self.COMP:
            if self.ncomp[en] > 0:
                tg[("c", en)] = self.ncomp[en]
        for q in self.DMAQ:
            n = self.ndma[q]
            for j in range(min(NDS, n)):
                tg[("d", q, j)] = 16 * ((n - j + NDS - 1) // NDS)
        for en in self.COMP + self.DMAQ:
            waits = []
            for sk, v in tg.items():
                if self.waited.get((en, sk), 0) < v:
                    self.waited[(en, sk)] = v
                    waits.append((sk, v))
                    self.semkeys.add(sk)
            self.ops[en].append((waits, None, None, 0))
        self.lastw.clear()
        self.rd.clear()

    def dram(self, name, shape, dt, out=False):
        kind = "ExternalOutput" if out else "ExternalInput"
        return self.nc.dram_tensor(name, list(shape), dt, kind=kind).ap()

    def sb(self, name, shape, dt):
        return self.es.enter_context(self.nc.sbuf_tensor(name, list(shape), dt))

    def scratch(self, name, shape, dt):
        return self.nc.dram_tensor(name, list(shape), dt).ap()

    def op(self, eng, fn, reads=(), writes=(), extra=()):
        is_dma = eng in self.DMAQ
        deps = [(t, "raw") for t in extra if t is not None]
        for k in reads:
            t = self.lastw.get(k)
            if t is not None:
                deps.append((t, "raw"))
        for k in writes:
            t = self.lastw.get(k)
            if t is not None:
                deps.append((t, "waw"))
            for t in self.rd.get(k, ()):
                deps.append((t, "war"))
        need = {}
        for (semkey, val, teng, tdma), kind in deps:
            if (not tdma) and (not is_dma) and teng == eng and kind != "raw":
                continue
            need[semkey] = max(need.get(semkey, 0), val)
        if is_dma:
            n = self.ndma[eng]
            self.ndma[eng] += 1
            j = n % NDS
            semkey = ("d", eng, j)
            val = 16 * (n // NDS + 1)
            if n >= NDS:
                need[semkey] = max(need.get(semkey, 0), val - 16)
            inc = 16
        else:
            self.ncomp[eng] += 1
            semkey = ("c", eng)
            val = self.ncomp[eng]
            inc = 1
        waits = []
        for sk, v in need.items():
            if self.waited.get((eng, sk), 0) >= v:
                continue
            self.waited[(eng, sk)] = v
            waits.append((sk, v))
            self.semkeys.add(sk)
        self.semkeys.add(semkey)
        self.ops[eng].append((waits, fn, semkey, inc))
        tok = (semkey, val, eng, is_dma)
        for k in writes:
            self.lastw[k] = tok
            self.rd[k] = []
        for k in reads:
            lst = self.rd.setdefault(k, [])
            if not is_dma:
                lst[:] = [t for t in lst if not (t[2] == eng and not t[3])]
            lst.append(tok)
        return tok

    def emit(self):
        nc = self.nc
        sems = {}
        for i, sk in enumerate(sorted(self.semkeys, key=str)):
            sems[sk] = self.es.enter_context(nc.semaphore("s%d" % i))
        ops = self.ops
        ndma = self.ndma

        def mk(eng):
            def body(e):
                for waits, fn, semkey, inc in ops[eng]:
                    for sk, v in waits:
                        e.wait_ge(sems[sk], v)
                    if fn is not None:
                        fn(e).then_inc(sems[semkey], inc)
                if eng in ndma:
                    n = ndma[eng]
                    for j in range(min(NDS, n)):
                        cnt = (n - j + NDS - 1) // NDS
                        e.wait_ge(sems[("d", eng, j)], 16 * cnt)
            return body

        with nc.Block() as block:
            block.tensor(mk("pe"))
            block.scalar(mk("act"))
            block.vector(mk("dve"))
            block.gpsimd(mk("pool"))
            block.sync(mk("sp"))
        self.es.close()
        return nc


def mm(out, lhsT, rhs, start, stop):
    return lambda e: e.matmul(out, lhsT=lhsT, rhs=rhs, start=start, stop=stop)


def dma(out, in_):
    return lambda e: e.dma_start(out=out, in_=in_)


def dmaf(out_fn, in_fn):
    return lambda e: e.dma_start(out=out_fn(e), in_=in_fn(e))


def memset(out, val):
    return lambda e: e.memset(out, val)


def act(out, in_, func, scale=None, bias=None):
    kw = {}
    if scale is not None:
        kw["scale"] = scale
    if bias is not None:
        kw["bias"] = bias
    return lambda e: e.activation(out=out, in_=in_, func=func, **kw)


def stt(out, in0, scalar, in1, op0, op1):
    return lambda e: e.scalar_tensor_tensor(out=out, in0=in0, scalar=scalar, in1=in1, op0=op0, op1=op1)


def tt(out, in0, in1, op):
    return lambda e: e.tensor_tensor(out=out, in0=in0, in1=in1, op=op)


def tcopy(out, in_):
    return lambda e: e.tensor_copy(out=out, in_=in_)


def acopy(out, in_):
    return lambda e: e.copy(out=out, in_=in_)


def recip(out, in_):
    return lambda e: e.reciprocal(out=out, in_=in_)


def tscal(out, in0, s1, s2, op0, op1):
    return lambda e: e.tensor_scalar(out=out, in0=in0, scalar1=s1, scalar2=s2, op0=op0, op1=op1)


def emit_rstd(P, ss_ps, sskey, tmp, tmpkey, inv_n):
    P.op("act", act(tmp, ss_ps, AF.Ln, scale=inv_n, bias=P.eps), reads=[sskey], writes=[tmpkey])
    P.op("act", act(tmp, tmp, AF.Exp, scale=-0.5), reads=[tmpkey], writes=[tmpkey])


def emit_xnorm(P, xT, xkey, hT, hkey, gcol, ones, sq, ss_ps, rstd, ntc):
    for tc in range(ntc):
        ts = slice(tc * 512, (tc + 1) * 512)
        for k in range(8):
            s = k % 2
            P.op("act", act(sq[s][:], xT[:, k, ts], AF.Square), reads=[(xkey, tc)], writes=[("sq", s)])
            P.op("pe", mm(ss_ps, ones, sq[s][:], k == 0, k == 7), reads=[("sq", s), "cst"], writes=["ss_ps"])
        emit_rstd(P, ss_ps, "ss_ps", rstd, "rstd", 1.0 / DM)
        for k in range(8):
            P.op("dve", stt(hT[:, k, ts], xT[:, k, ts], gcol[:, k:k + 1], rstd, ALU.mult, ALU.mult),
                 reads=[(xkey, tc), "rstd", "gcol"], writes=[(hkey, tc)])


NQK = 25
ROPE_CH = (20, 21, 22, 23, 24)
NVH = 22
VW = NVH * 65
NCLS = 5
TB = 1024
Q_CHUNKS = [0, 1, 2, 3, 4, 5, 12, 13, 14, 15, 20, 21, 22, 23]
K_CHUNKS = [6, 7, 8, 9, 10, 11, 16, 17, 18, 19, 24]
OCH = ((0, 2), (2, 6), (6, 10))


class Holder:
    pass


def setup(P):
    D = Holder()
    D.xT = P.dram("xT", [DM, SEQ], F32)
    D.g1 = P.dram("g1c", [128, 2, 8], F32)
    D.g2 = P.dram("g2c", [128, 2, 8], F32)
    D.gq = P.dram("gq", [128, 2, NQK], F32)
    D.wqk = P.dram("wqk", [2, NQK, 128, 8, 128], F32)
    D.wv = P.dram("wv", [2, 3, 128, 8, 512], F32)
    D.cos = P.dram("cosT", [128, SEQ], F32)
    D.sin = P.dram("sinT", [128, SEQ], F32)
    D.cstd = P.dram("cst", [3, 128, 128], F32)
    D.abias = P.dram("abias", [4, 3, 128, 256], F32)
    D.bbias = P.dram("bbias", [2, 4, 2, NCLS, 128, 640], F32)
    D.seld = P.dram("sel", [65, 64], F32)
    D.wg = P.dram("wg", [2, 8, 128, 3, 8, 128], F32)
    D.pw = P.dram("pw", [2, 8, 128, 10, 128], F32)
    D.wo = P.dram("wo", [2, 8, 128, 8, 128], F32)
    D.wup = P.dram("wup", [2, 22, 128, 2, 8, 128], F32)
    D.wdn = P.dram("wdn", [2, 8, 128, 22, 128], F32)
    D.bbias2 = P.dram("bbias2", [4, 2, NCLS, 128, 768], F32)
    D.y = P.dram("yT", [DM, NT], F32, out=True)
    D.q2_scr = P.scratch("q2_scr", [len(Q_CHUNKS) * 128, NT], BF16)
    D.o2_scr = P.scratch("o2_scr", [1280, NT], BF16)
    D.xo_scr = P.scratch("xo_scr", [DM, NT], F32)
    D.coso = P.scratch("cos_own", [128, NT], F32)
    D.sino = P.scratch("sin_own", [128, NT], F32)
    D.kwA = P.scratch("kwA_scr", [768, NT + 2 * PADT], BF16)
    D.kwB = P.scratch("kwB_scr", [512, 2560], BF16)
    D.vw = P.scratch("vw_scr", [NT + 2 * PADT + 16, VW], BF16)
    D.qk_scr = P.scratch("qk_scr", [NQK * 128, SEQ + 2 * PADT], BF16)
    D.v_scr = P.scratch("v_scr", [SEQ + 2 * PADT + 16, VW], BF16)
    D.o_scr = P.scratch("o_scr", [1280, SEQ], BF16)
    D.x_scr = P.scratch("x_scr", [DM, SEQ], F32)
    D.cst = P.sb("cst_sb", [128, 3, 128], BF16)
    D.sel = P.sb("sel_sb", [65, 64], F32)
    D.g1s = P.sb("g1_sb", [128, 2, 8], F32)
    D.g2s = P.sb("g2_sb", [128, 2, 8], F32)
    D.gqs = P.sb("gq_sb", [128, 2, NQK], F32)
    return D


def phase_init(P, D):
    P.arena_reset()
    z = P.a16([128, 8, VW])
    P.op("dve", memset(z, 0.0), writes=["z"])
    P.op("dve", memset(P.eps, EPS), writes=["eps"])
    P.op("pool", dma(D.cst[:], D.cstd.rearrange("a p c -> p a c")), writes=["cst"])
    P.op("sp", dma(D.sel[:], D.seld[:, :]), writes=["sel"])
    P.op("sp", dma(D.g1s[:], D.g1[:, :, :]), writes=["g1"])
    P.op("sp", dma(D.g2s[:], D.g2[:, :, :]), writes=["g2"])
    P.op("sp", dma(D.gqs[:], D.gq[:, :, :]), writes=["gq"])
    for r0 in (0, PADT + SEQ):
        P.op("sp", dma(D.v_scr[r0:r0 + 1024, :].rearrange("(t p) c -> p t c", p=128), z), reads=["z"])
    P.op("sp", dma(D.v_scr[SEQ + 2 * PADT:SEQ + 2 * PADT + 16, :], z[0:16, 0, :]), reads=["z"])
    zf = z.rearrange("p t c -> p (t c)")
    for cc in list(range(6, 12)) + list(range(16, 20)) + [24]:
        for c0 in (0, PADT + SEQ):
            P.op("sp", dma(D.qk_scr[cc * 128:(cc + 1) * 128, c0:c0 + PADT], zf[:, 0:PADT]), reads=["z"])


def phase_L1(P, D, l, x_src, blocks, chunks=None, do_v=True, NB=1024, own=False):
    P.barrier()
    P.arena_reset()
    chunks = list(range(NQK)) if chunks is None else list(chunks)
    ntc = NB // 512
    xT = P.a32([128, 8, NB])
    cosT = P.a32([128, NB])
    sinT = P.a32([128, NB])
    rstd = P.a32([128, 512])
    r32 = [P.a32([128, 512]) for _ in range(2)]
    t1 = [P.a32([128, 512]) for _ in range(2)]
    t2 = [P.a32([128, 512]) for _ in range(2)]
    hT = P.a16([128, 8, NB])
    wv = P.a16([128, 3, 8, 512])
    NW = 3
    w = [P.a16([128, 8, 128]) for _ in range(NW)]
    sq = [P.a16([128, 512]) for _ in range(2)]
    qnb = [P.a16([128, 512]) for _ in range(2)]
    ob = [P.a16([128, NB]) for _ in range(2)]
    vb = [P.a16([128, NVH, 65]) for _ in range(2)]
    ss_ps = P.bank[0][:]
    z_ps = [P.bank[1], P.bank[2], P.bank[6], P.bank[7]]
    st_ps = [P.bank[3], P.bank[4]]
    pm_ps = P.bank[5]
    v_ps = [P.bank[6], P.bank[7]]
    NZ = 4
    ones, blk, perm = D.cst[:, 0, :], D.cst[:, 1, :], D.cst[:, 2, :]
    g1 = D.g1s[:, l, :]
    gq = D.gqs[:, l, :]

    def L1_block(c0):
        for tc in range(ntc):
            P.op("sp", dma(xT[:, :, tc * 512:(tc + 1) * 512], xv[:, :, c0 + tc * 512:c0 + (tc + 1) * 512]),
                 writes=[("x", tc)])
        P.op("sp", dma(cosT, cos_src[:, c0:c0 + NB]), writes=["cos"])
        P.op("sp", dma(sinT, sin_src[:, c0:c0 + NB]), writes=["sin"])

        emit_xnorm(P, xT, "x", hT, "h", g1, ones, sq, ss_ps, rstd, ntc)

        units = [(cc, tc) for cc in chunks for tc in range(ntc)]

        def stageA(u):
            cc, tc = units[u]
            ws = (u // ntc) % NW
            if tc == 0:
                P.op("pool", dma(w[ws], D.wqk[l, cc]), writes=[("w", ws)])
            zb = u % NZ
            for k in range(8):
                P.op("pe", mm(z_ps[zb][:], w[ws][:, k, :], hT[:, k, tc * 512:(tc + 1) * 512], k == 0, k == 7),
                     reads=[("w", ws), ("h", tc)], writes=[("z", zb)])

        def stageB1(u):
            zb = u % NZ
            sb_ = u % 2
            P.op("act", act(sq[sb_], z_ps[zb][:], AF.Square), reads=[("z", zb)], writes=[("sq", sb_)])
            P.op("pe", mm(st_ps[sb_][:], blk, sq[sb_], True, True), reads=[("sq", sb_), "cst"], writes=[("st", sb_)])

        def stageB2(u):
            cc, tc = units[u]
            zb = u % NZ
            sb_ = u % 2
            ts = slice(tc * 512, (tc + 1) * 512)
            osl = (u // ntc) % 2
            emit_rstd(P, st_ps[sb_][:], ("st", sb_), r32[sb_], ("r32", sb_), 1.0 / 64)
            rope = cc in ROPE_CH
            dst = qnb[sb_] if rope else ob[osl][:, ts]
            dkey = ("qnb", sb_) if rope else ("ob", osl)
            P.op("dve", stt(dst, z_ps[zb][:], gq[:, cc:cc + 1], r32[sb_], ALU.mult, ALU.mult),
                 reads=[("z", zb), ("r32", sb_), "gq"], writes=[dkey])
            if rope:
                P.op("pe", mm(pm_ps[:], perm, qnb[sb_], True, True), reads=[("qnb", sb_), "cst"], writes=["pm"])
                P.op("dve", tt(t1[sb_], qnb[sb_], cosT[:, ts], ALU.mult), reads=[("qnb", sb_), "cos"], writes=[("t1", sb_)])
                P.op("dve", tt(t2[sb_], pm_ps[:], sinT[:, ts], ALU.mult), reads=["pm", "sin"], writes=[("t2", sb_)])
                P.op("dve", tt(ob[osl][:, ts], t1[sb_], t2[sb_], ALU.add),
                     reads=[("t1", sb_), ("t2", sb_)], writes=[("ob", osl)])
            if tc == ntc - 1:
                if own:
                    qi = Q_CHUNKS.index(cc)
                    P.op("sp", dma(D.q2_scr[qi * 128:(qi + 1) * 128, c0:c0 + NB], ob[osl]), reads=[("ob", osl)])
                else:
                    P.op("sp", dma(D.qk_scr[cc * 128:(cc + 1) * 128, PADT + c0:PADT + c0 + NB], ob[osl]),
                         reads=[("ob", osl)])

        n = len(units)
        for i in range(n + 2):
            if i < n:
                stageA(i)
            if 1 <= i <= n:
                stageB1(i - 1)
            if i >= 2:
                stageB2(i - 2)

        if do_v:
            for tk in range(NB // 128):
                vs = tk % 2
                for g in range(3):
                    nh = 8 if g < 2 else 6
                    pb = (tk * 3 + g) % 2
                    for k in range(8):
                        P.op("pe", mm(v_ps[pb][:, 0:nh * 64], hT[:, k, tk * 128:(tk + 1) * 128], wv[:, g, k, 0:nh * 64],
                                      k == 0, k == 7), reads=[("h", tk // 4), ("wv", g)], writes=[("z", 2 + pb)])
                    src = v_ps[pb][:, 0:nh * 64].rearrange("p (h c) -> p h c", c=64)
                    dst = vb[vs][:, 8 * g:8 * g + nh, 0:64]
                    if g % 2 == 0:
                        P.op("act", acopy(dst, src), reads=[("z", 2 + pb)], writes=[("vb", vs)])
                    else:
                        P.op("dve", tcopy(dst, src), reads=[("z", 2 + pb)], writes=[("vb", vs)])
                r0 = PADT + c0 + tk * 128
                P.op("sp", dma(D.v_scr[r0:r0 + 128, :], vb[vs].rearrange("p h c -> p (h c)")), reads=[("vb", vs)])

    xv = x_src.rearrange("(k p) t -> p k t", p=128)
    cos_src, sin_src = (D.coso, D.sino) if own else (D.cos, D.sin)
    if do_v:
        for g in range(3):
            P.op("pool", dma(wv[:, g], D.wv[l, g]), writes=[("wv", g)])
        for i in range(2):
            P.op("dve", memset(vb[i][:, :, 64:65], 1.0), writes=[("vb", i)])
    ucount = [0]
    for c0 in blocks:
        L1_block(c0)


def b_tiles(n):
    if n <= 1:
        return [0, 1, 2, 3], 1 + n
    if n >= 62:
        return [60, 61, 62, 63], 3 + (n - 62)
    return [n - 2, n - 1, n, n + 1, n + 2], 0


def b2_tiles(nl):
    if nl == 0:
        return [0, 1, 2, 3, 4, 5], 0
    if nl == 1:
        return [1, 2, 3, 4, 5], 1
    if nl == 14:
        return [14, 15, 16, 17, 18], 3
    if nl == 15:
        return [14, 15, 16, 17, 18, 19], 4
    return [nl, nl + 1, nl + 2, nl + 3, nl + 4], 2


def norm_out(P, stg_ap, stgkey, osb_ap, osbkey, sel, den_ps, rden, out_ap, use_act):
    P.op("pe", mm(den_ps[0:64, :], sel, stg_ap[0:65, :], True, True), reads=[stgkey, "sel"], writes=["den"])
    if use_act:
        P.op("act", act(rden[0:64, :], den_ps[0:64, :], AF.Ln), reads=["den"], writes=["rden"])
        P.op("act", act(rden[0:64, :], rden[0:64, :], AF.Exp, scale=-1.0), reads=["rden"], writes=["rden"])
    else:
        P.op("dve", recip(rden[0:64, :], den_ps[0:64, :]), reads=["den"], writes=["rden"])
    P.op("dve", tt(osb_ap, stg_ap[0:64, :], rden[0:64, :], ALU.mult), reads=[stgkey, "rden"], writes=[osbkey])
    P.op("sp", dma(out_ap, osb_ap), reads=[osbkey])


def vload(P, v_sb, slot, base, src, nt):
    t = 0
    while t < nt:
        a = base + t
        n = min(nt - t, 16 - a % 16)
        P.op("sp", dma(v_sb[:, slot, a:a + n, :], src[:, t:t + n, :]), writes=[("v", slot, a // 16)])
        t += n


def phase_att(P, D, l, rp, own):
    P.barrier()
    P.arena_reset()
    NQ = NT if own else SEQ
    KW = NQ + 2 * PADT
    BW = 768 if own else 640
    q_sb = P.a16([128, NQ])
    k_sb = P.a16([128, SEQ + 2 * PADT]) if not own else P.a16([128, SEQ])
    v_sb = P.a16([128, 2, 80, 65])
    pT = [P.a16([128, 768]) for _ in range(4)]
    pTC = [P.a16([128, 1024]) for _ in range(2)]
    pTA = [P.a16([128, 256]) for _ in range(8)]
    osb = [P.a16([128, 512]) for _ in range(2)]
    abias = P.a32([128, 3, 256])
    bbias = P.a32([128, 2, NCLS, BW])
    oacc = P.a32([65, NQ])
    sc = [P.a32([128, 768]) for _ in range(2)]
    scA = [P.a32([128, 256]) for _ in range(8)]
    stg = [P.a32([65, 512]) for _ in range(2)]
    rden = P.a32([64, 512])
    s_ps = [P.bank[i] for i in range(4)]
    o_ps = [P.bank[4], P.bank[5]]
    den_ps = P.bank[6]
    sel = D.sel[:]
    qk = D.qk_scr
    q2 = D.q2_scr
    v3 = D.v_scr.rearrange("t (h c) -> t h c", c=65)
    va3 = D.vw.rearrange("t (h c) -> t h c", c=65) if own else v3
    o_dst = D.o2_scr if own else D.o_scr

    P.op("sp", dma(abias, D.abias[rp].rearrange("g p c -> p g c")), writes=["abias"])
    for hi in range(2):
        bsrc = D.bbias2[rp, hi] if own else D.bbias[l, rp, hi]
        P.op("sp", dma(bbias[:, hi], bsrc.rearrange("c p x -> p c x")), writes=[("bbias", hi)])

    ro = 64 * (rp % 2)
    sreg = [s_ps[i][:, 0:256] for i in range(4)]
    oreg = [P.bank[4 + i][0:65, 0:128] for i in range(4)]
    for g, rate in enumerate(A_RATES):
        if g > 0:
            P.barrier()
        L_ = NQ // rate
        nqb = L_ // 128
        qrow = (2 * g + rp // 2) * 128 + ro
        if own:
            P.op("sp", dma(q_sb[0:64, :], q2[qrow:qrow + 64, :]), writes=["q"])
            P.op("sp", dma(k_sb[0:64, 0:KW], D.kwA[qrow:qrow + 64, :]), writes=["k"])
        else:
            krow = (6 + 2 * g + rp // 2) * 128 + ro
            P.op("sp", dma(q_sb[0:64, :], qk[qrow:qrow + 64, PADT:PADT + SEQ]), writes=["q"])
            P.op("sp", dma(k_sb[0:64, 0:KW], qk[krow:krow + 64, :]), writes=["k"])
        head = g * 4 + rp
        M = L_ + 128
        for res in range(rate):
            start = PADT - 64 * rate + res
            src = va3[start:start + rate * M].rearrange("(m r) h c -> m r h c", r=rate)[:, 0, head, :]
            src = src.rearrange("(t p) c -> p t c", p=128)
            vload(P, v_sb, 0, res * (nqb + 1), src, nqb + 1)
        qv = q_sb[0:64, :].rearrange("p (l r) -> p r l", r=rate)
        kv = k_sb[0:64, 0:KW].rearrange("p (l r) -> p r l", r=rate)
        ov = oacc.rearrange("p (l r) -> p r l", r=rate)
        koff = PADT // rate
        units = [(res, qb) for res in range(rate) for qb in range(nqb)]

        def A1(u):
            res, qb = units[u]
            i4 = u % 4
            qs = qv[:, res, 128 * qb:128 * qb + 128]
            l0 = koff + 128 * qb - 64
            P.op("pe", mm(sreg[i4][:, 0:128], kv[:, res, l0:l0 + 128], qs, True, True),
                 reads=["q", "k"], writes=[("s", i4)])
            P.op("pe", mm(sreg[i4][:, 128:256], kv[:, res, l0 + 128:l0 + 256], qs, True, True),
                 reads=["q", "k"], writes=[("s", i4)])

        def A2(u):
            i4 = u % 4
            i8 = u % 8
            P.op("dve", stt(scA[i8], sreg[i4], 0.125, abias[:, g, :], ALU.mult, ALU.add),
                 reads=[("s", i4), "abias"], writes=[("scA", i8)])
            P.op("act", act(pTA[i8], scA[i8], AF.Exp), reads=[("scA", i8)], writes=[("pTA", i8)])

        def A3(u):
            res, qb = units[u]
            i4 = u % 4
            i8 = u % 8
            tA = res * (nqb + 1) + qb
            P.op("pe", mm(oreg[i4], v_sb[:, 0, tA, :], pTA[i8][:, 0:128], True, False),
                 reads=[("pTA", i8), ("v", 0, tA // 16)], writes=[("oA", i4)])
            P.op("pe", mm(oreg[i4], v_sb[:, 0, tA + 1, :], pTA[i8][:, 128:256], False, True),
                 reads=[("pTA", i8), ("v", 0, (tA + 1) // 16)], writes=[("oA", i4)])
            dst = ov[:, res, 128 * qb:128 * qb + 128]
            if g == 0:
                P.op("dve", tcopy(dst, oreg[i4]), reads=[("oA", i4)], writes=["oacc"])
            else:
                P.op("dve", tt(dst, oreg[i4], dst, ALU.add), reads=[("oA", i4), "oacc"], writes=["oacc"])

        n = len(units)
        for i in range(n + 3):
            if i < n:
                A1(i)
            if 1 <= i < n + 1:
                A2(i - 1)
            if 3 <= i:
                A3(i - 3)
    P.barrier()
    for c in range(NQ // 512):
        cs = slice(c * 512, (c + 1) * 512)
        o_ = c % 2
        norm_out(P, oacc[:, cs], "oacc", osb[o_][0:64, :], ("osb", o_), sel, den_ps, rden,
                 o_dst[rp * 64:rp * 64 + 64, cs], use_act=True)

    P.barrier()
    if own:
        P.op("sp", dma(q_sb, q2[(6 + rp) * 128:(7 + rp) * 128, :]), writes=["q"])
        P.op("sp", dma(k_sb[:, 0:2560], D.kwB[rp * 128:(rp + 1) * 128, :]), writes=["k"])
        for hi in range(2):
            src = va3[PADT - 256:PADT - 256 + 2560, 12 + 2 * rp + hi, :].rearrange("(t p) c -> p t c", p=128)
            vload(P, v_sb, hi, 0, src, 20)
        tiles_fn = b2_tiles
    else:
        P.op("sp", dma(q_sb, qk[(12 + rp) * 128:(13 + rp) * 128, PADT:PADT + SEQ]), writes=["q"])
        P.op("sp", dma(k_sb[:, 0:SEQ], qk[(16 + rp) * 128:(17 + rp) * 128, PADT:PADT + SEQ]), writes=["k"])
        for hi in range(2):
            src = v3[PADT:PADT + SEQ, 12 + 2 * rp + hi, :].rearrange("(t p) c -> p t c", p=128)
            vload(P, v_sb, hi, 0, src, 64)
        tiles_fn = b_tiles
    unitsB = [(c, j, hi) for c in range(NQ // 512) for j in range(4) for hi in range(2)]
    pending = []

    def B1(u):
        c, j, hi = unitsB[u]
        n_ = 4 * c + j
        kts, cls = tiles_fn(n_)
        hp = slice(64 * hi, 64 * hi + 64)
        qs = q_sb[hp, 128 * n_:128 * n_ + 128]
        sa, sb2 = s_ps[2 * (u % 2)], s_ps[2 * (u % 2) + 1]
        for idx, kt in enumerate(kts):
            dst = sa[:, idx * 128:(idx + 1) * 128] if idx < 4 else sb2[:, (idx - 4) * 128:(idx - 3) * 128]
            P.op("pe", mm(dst, k_sb[hp, 128 * kt:128 * kt + 128], qs, True, True),
                 reads=["q", "k"], writes=[("s", 2 * (u % 2) + (0 if idx < 4 else 1))])

    def B2(u):
        c, j, hi = unitsB[u]
        kts, cls = tiles_fn(4 * c + j)
        b2 = u % 2
        p4 = u % 4
        sa, sb2 = s_ps[2 * b2], s_ps[2 * b2 + 1]
        nk = len(kts)
        P.op("dve", stt(sc[b2][:, 0:512], sa[:, :], 0.125, bbias[:, hi, cls, 0:512], ALU.mult, ALU.add),
             reads=[("s", 2 * b2), ("bbias", hi)], writes=[("sc", b2)])
        if nk > 4:
            w2 = (nk - 4) * 128
            P.op("dve", stt(sc[b2][:, 512:512 + w2], sb2[:, 0:w2], 0.125, bbias[:, hi, cls, 512:512 + w2],
                            ALU.mult, ALU.add), reads=[("s", 2 * b2 + 1), ("bbias", hi)], writes=[("sc", b2)])
        P.op("act", act(pT[p4][:, 0:nk * 128], sc[b2][:, 0:nk * 128], AF.Exp), reads=[("sc", b2)], writes=[("pT", p4)])

    def B3(u):
        for due, fn in list(pending):
            if due <= u:
                fn()
                pending.remove((due, fn))
        c, j, hi = unitsB[u]
        kts, cls = tiles_fn(4 * c + j)
        p4 = u % 4
        nk = len(kts)
        for idx, kt in enumerate(kts):
            P.op("pe", mm(o_ps[hi][0:65, j * 128:(j + 1) * 128], v_sb[:, hi, kt, :], pT[p4][:, idx * 128:(idx + 1) * 128],
                          idx == 0, idx == nk - 1), reads=[("pT", p4), ("v", hi, kt // 16)], writes=[("o", hi)])
        if j == 3:
            P.op("act", acopy(stg[hi], o_ps[hi][0:65, :]), reads=[("o", hi)], writes=[("stg", hi)])
            r0 = 256 + rp * 128 + 64 * hi
            pending.append((u + 3, (lambda hi=hi, r0=r0, c=c: norm_out(
                P, stg[hi], ("stg", hi), osb[hi][0:64, :], ("osb", hi), sel, den_ps, rden,
                o_dst[r0:r0 + 64, c * 512:(c + 1) * 512], use_act=True))))

    n = len(unitsB)
    for i in range(n + 2):
        if i < n:
            B1(i)
        if 1 <= i <= n:
            B2(i - 1)
        if i >= 2:
            B3(i - 2)
    for due, fn in pending:
        fn()
    pending.clear()

    P.barrier()
    if own:
        P.op("sp", dma(q_sb, q2[(10 + rp) * 128:(11 + rp) * 128, :]), writes=["q"])
    else:
        P.op("sp", dma(q_sb, qk[(20 + rp) * 128:(21 + rp) * 128, PADT:PADT + SEQ]), writes=["q"])
    kr2 = 24 * 128 + 64 * (rp // 2)
    for hi in range(2):
        P.op("sp", dma(k_sb[64 * hi:64 * hi + 64, 0:SEQ], qk[kr2:kr2 + 64, PADT:PADT + SEQ]), writes=[("k", hi)])
    src = v3[PADT:PADT + SEQ, 20 + rp // 2, :].rearrange("(t p) c -> p t c", p=128)
    vload(P, v_sb, 0, 0, src, 64)
    unitsC = [(c, kt) for c in range(NQ // 512) for kt in range(64)]

    def C1(u):
        c, kt = unitsC[u]
        for hi in range(2):
            hp = slice(64 * hi, 64 * hi + 64)
            sb_ = 2 * (u % 2) + hi
            P.op("pe", mm(s_ps[sb_][:, :], k_sb[hp, 128 * kt:128 * kt + 128], q_sb[hp, c * 512:(c + 1) * 512], True, True),
                 reads=["q", ("k", hi)], writes=[("s", sb_)])

    def C2(u):
        c, kt = unitsC[u]
        if kt == 6:
            for fn in pending:
                fn()
            pending.clear()
        b2 = u % 2
        sb0 = 2 * b2
        P.op("act", act(pTC[b2], P.psall[:, 512 * sb0:512 * sb0 + 1024], AF.Exp, scale=0.125),
             reads=[("s", sb0), ("s", sb0 + 1)], writes=[("pTC", b2)])
        for hi in range(2):
            P.op("pe", mm(o_ps[hi][0:65, :], v_sb[:, 0, kt, :], pTC[b2][:, 512 * hi:512 * (hi + 1)], kt == 0, kt == 63),
                 reads=[("pTC", b2), ("v", 0, kt // 16)], writes=[("o", hi)])
            if kt == 63:
                P.op("dve", tcopy(stg[hi], o_ps[hi][0:65, :]), reads=[("o", hi)], writes=[("stg", hi)])
                r0 = 768 + rp * 128 + 64 * hi
                pending.append((lambda hi=hi, r0=r0, c=c: norm_out(
                    P, stg[hi], ("stg", hi), osb[hi][0:64, :], ("osb", hi), sel, den_ps, rden,
                    o_dst[r0:r0 + 64, c * 512:(c + 1) * 512], use_act=False)))

    n = len(unitsC)
    for i in range(n + 1):
        if i < n:
            C1(i)
        if i >= 1:
            C2(i - 1)
    for fn in pending:
        fn()
    pending.clear()


def phase_L2(P, D, l, x_src, blocks, dst, own=False):
    P.barrier()
    P.arena_reset()
    xT = P.a32([128, 8, TB])
    rstd = P.a32([128, 512])
    gs = [P.a32([128, 512]) for _ in range(2)]
    macc = [P.a32([128, 512]) for _ in range(2)]
    mtmp = [P.a32([128, 512]) for _ in range(2)]
    hT = P.a16([128, 8, TB])
    aT = P.a16([128, 22, TB])
    sq = [P.a16([128, 512]) for _ in range(2)]
    oT = P.a16([128, 10, TB])
    mT = P.a16([128, 8, TB])
    wg = [P.a16([128, 3, 8, 128]) for _ in range(2)]
    pw = [P.a16([128, 10, 128]) for _ in range(2)]
    wo = [P.a16([128, 8, 128]) for _ in range(2)]
    wup = [P.a16([128, 2, 8, 128]) for _ in range(2)]
    mflat = mT.rearrange("p k t -> p (k t)")
    wdn = [mflat[:, i * 2816:(i + 1) * 2816].rearrange("p (f c) -> p f c", c=128) for i in range(2)]
    last = {"wo": None, "dn": None}
    ss_ps = P.bank[0][:]
    g_ps = [P.bank[1], P.bank[2]]
    p_ps = [P.bank[3], P.bank[4]]
    y_ps = [P.bank[5], P.bank[6]]
    ones = D.cst[:, 0, :]
    g1 = D.g1s[:, l, :]
    g2 = D.g2s[:, l, :]
    xv = x_src.rearrange("(k p) t -> p k t", p=128)
    ov = (D.o2_scr if own else D.o_scr).rearrange("(j p) t -> p j t", p=128)
    yv = dst.rearrange("(k p) t -> p k t", p=128)
    cnt = {"g": 0, "y": 0, "m": 0}

    for c0 in blocks:
        for tc in range(2):
            ts = slice(tc * 512, (tc + 1) * 512)
            P.op("sp", dma(xT[:, :, ts], xv[:, :, c0 + tc * 512:c0 + (tc + 1) * 512]), writes=[("x", tc)])
        for tc in range(2):
            ts = slice(tc * 512, (tc + 1) * 512)
            P.op("sp", dma(oT[:, :, ts], ov[:, :, c0 + tc * 512:c0 + (tc + 1) * 512]), writes=[("o", tc)])
        emit_xnorm(P, xT, "x", hT, "h", g1, ones, sq, ss_ps, rstd, 2)

        for dc in range(8):
            ws = dc % 2
            P.op("pool", dma(wg[ws], D.wg[l, dc]), writes=[("wg", ws)])
            P.op("pool", dma(pw[ws], D.pw[l, dc]), writes=[("pw", ws)])
            for tc in range(2):
                ts = slice(tc * 512, (tc + 1) * 512)
                ma = cnt["m"] % 2
                cnt["m"] += 1
                for br in range(3):
                    gb = cnt["g"] % 2
                    cnt["g"] += 1
                    for k in range(8):
                        P.op("pe", mm(g_ps[gb][:], wg[ws][:, br, k, :], hT[:, k, ts], k == 0, k == 7),
                             reads=[("wg", ws), ("h", tc)], writes=[("g_ps", gb)])
                    P.op("act", act(gs[gb], g_ps[gb][:], AF.Sigmoid), reads=[("g_ps", gb)], writes=[("gs", gb)])
                    j0, j1 = OCH[br]
                    for j in range(j0, j1):
                        P.op("pe", mm(p_ps[gb][:], pw[ws][:, j, :], oT[:, j, ts], j == j0, j == j1 - 1),
                             reads=[("pw", ws), ("o", tc)], writes=[("p_ps", gb)])
                    if br == 0:
                        P.op("dve", tt(macc[ma], p_ps[gb][:], gs[gb], ALU.mult),
                             reads=[("p_ps", gb), ("gs", gb)], writes=[("macc", ma)])
                    else:
                        P.op("dve", tt(mtmp[ma], p_ps[gb][:], gs[gb], ALU.mult),
                             reads=[("p_ps", gb), ("gs", gb)], writes=[("mtmp", ma)])
                        d2 = macc[ma] if br == 1 else mT[:, dc, ts]
                        dkey = ("macc", ma) if br == 1 else ("m", tc)
                        P.op("dve", tt(d2, macc[ma], mtmp[ma], ALU.add),
                             reads=[("macc", ma), ("mtmp", ma)], writes=[dkey],
                             extra=[last["dn"]] if br == 2 else ())

        for oc in range(8):
            ws = oc % 2
            P.op("pool", dma(wo[ws], D.wo[l, oc]), writes=[("wo", ws)])
            for tc in range(2):
                ts = slice(tc * 512, (tc + 1) * 512)
                yb = cnt["y"] % 2
                cnt["y"] += 1
                for k in range(8):
                    last["wo"] = P.op("pe", mm(y_ps[yb][:], wo[ws][:, k, :], mT[:, k, ts], k == 0, k == 7),
                                      reads=[("wo", ws), ("m", tc)], writes=[("y_ps", yb)])
                P.op("dve", tt(xT[:, oc, ts], y_ps[yb][:], xT[:, oc, ts], ALU.add),
                     reads=[("y_ps", yb), ("x", tc)], writes=[("x", tc)])

        emit_xnorm(P, xT, "x", hT, "h", g2, ones, sq, ss_ps, rstd, 2)
        for fc in range(22):
            ws = fc % 2
            P.op("pool", dma(wup[ws], D.wup[l, fc]), writes=[("wup", ws)])
            for tc in range(2):
                ts = slice(tc * 512, (tc + 1) * 512)
                gb = cnt["g"] % 2
                cnt["g"] += 1
                for k in range(8):
                    P.op("pe", mm(g_ps[gb][:], wup[ws][:, 0, k, :], hT[:, k, ts], k == 0, k == 7),
                         reads=[("wup", ws), ("h", tc)], writes=[("g_ps", gb)])
                for k in range(8):
                    P.op("pe", mm(p_ps[gb][:], wup[ws][:, 1, k, :], hT[:, k, ts], k == 0, k == 7),
                         reads=[("wup", ws), ("h", tc)], writes=[("p_ps", gb)])
                P.op("act", act(gs[gb], g_ps[gb][:], AF.Silu), reads=[("g_ps", gb)], writes=[("gs", gb)])
                P.op("dve", tt(aT[:, fc, ts], p_ps[gb][:], gs[gb], ALU.mult),
                     reads=[("p_ps", gb), ("gs", gb)], writes=[("a", tc)])
        for oc in range(8):
            ws = oc % 2
            P.op("pool", dma(wdn[ws], D.wdn[l, oc]), writes=[("wdn", ws)], extra=[last["wo"]])
            for tc in range(2):
                ts = slice(tc * 512, (tc + 1) * 512)
                yb = cnt["y"] % 2
                cnt["y"] += 1
                for f in range(22):
                    last["dn"] = P.op("pe", mm(y_ps[yb][:], wdn[ws][:, f, :], aT[:, f, ts], f == 0, f == 21),
                                      reads=[("wdn", ws), ("a", tc)], writes=[("y_ps", yb)])
                P.op("dve", tt(xT[:, oc, ts], y_ps[yb][:], xT[:, oc, ts], ALU.add),
                     reads=[("y_ps", yb), ("x", tc)], writes=[("x", tc)])
        for tc in range(2):
            ts = slice(tc * 512, (tc + 1) * 512)
            P.op("sp", dma(yv[:, :, c0 + tc * 512:c0 + (tc + 1) * 512], xT[:, :, ts]), reads=[("x", tc)])


def phase_windows(P, D):
    P.barrier()

    def own(e):
        return P.own(e)
    xs = D.x_scr
    P.op("sp", dmaf(lambda e: D.xo_scr[:, :], lambda e: xs[:, bass.ds(own(e), NT)]))
    P.op("sp", dmaf(lambda e: D.coso[:, :], lambda e: D.cos[:, bass.ds(own(e), NT)]))
    P.op("sp", dmaf(lambda e: D.sino[:, :], lambda e: D.sin[:, bass.ds(own(e), NT)]))
    P.op("sp", dmaf(lambda e: D.kwA[:, :], lambda e: D.qk_scr[6 * 128:12 * 128, bass.ds(own(e), NT + 2 * PADT)]))
    P.op("sp", dmaf(lambda e: D.kwB[:, :],
                    lambda e: D.qk_scr[16 * 128:20 * 128, PADT - 256:][:, bass.ds(own(e), 2560)]))
    P.op("sp", dmaf(lambda e: D.vw[:, :], lambda e: D.v_scr[bass.ds(own(e), NT + 2 * PADT + 16), :]))


def build_program():
    P = Prog()
    D = setup(P)
    phase_init(P, D)
    allblk = [b * 1024 for b in range(SEQ // 1024)]
    ownblk = [b * 1024 for b in range(NT // 1024)]
    phase_L1(P, D, 0, D.xT, allblk)
    for rp in range(4):
        phase_att(P, D, 0, rp, own=False)
    phase_L2(P, D, 0, D.xT, allblk, D.x_scr)
    phase_L1(P, D, 1, D.x_scr, allblk, chunks=K_CHUNKS, do_v=True)
    phase_windows(P, D)
    phase_L1(P, D, 1, D.xo_scr, ownblk, chunks=Q_CHUNKS, do_v=False, own=True)
    for rp in range(4):
        phase_att(P, D, 1, rp, own=True)
    phase_L2(P, D, 1, D.xo_scr, ownblk, D.y, own=True)
    return P.emit()


QK_COLS = ([128 * j for j in range(6)] + [768 + 128 * j for j in range(6)] + [2304 + 128 * j for j in range(4)]
           + [2816 + 128 * j for j in range(4)] + [3840 + 128 * j for j in range(4)] + [4352])
QK_GAIN = [0] * 6 + [1] * 6 + [2] * 4 + [3] * 4 + [4] * 4 + [5]


def chunk_major(w, cols, width):
    K = w.shape[0]
    wk = w.reshape(K // 128, 128, w.shape[1])
    return np.ascontiguousarray(np.stack([wk[:, :, c0:c0 + width].transpose(1, 0, 2) for c0 in cols], 0))


def t5_bucket_np(rel):
    half, max_exact = 16, 8
    ret = np.where(rel > 0, half, 0)
    n = np.abs(rel)
    nf = np.maximum(n, 1).astype(np.float32)
    large = max_exact + (np.log(nf / np.float32(max_exact)) / np.float32(np.log(1024 / max_exact))
                         * np.float32(half - max_exact)).astype(np.int32)
    large = np.minimum(large, half - 1)
    return ret + np.where(n < max_exact, n, large)


def rope_tables():
    t = np.arange(SEQ)
    inv = (np.float32(10000.0) ** (-np.arange(0, 32, 2, dtype=np.float32) / np.float32(32))).astype(np.float32)
    ang_r = (t // 64).astype(np.float32)[:, None] * inv[None, :]
    ang_c = (t % 64).astype(np.float32)[:, None] * inv[None, :]
    cr, sr, cc_, sc_ = np.cos(ang_r), np.sin(ang_r), np.cos(ang_c), np.sin(ang_c)
    cosT = np.concatenate([cr, cr, cc_, cc_], 1).T
    sinT = np.concatenate([-sr, sr, -sc_, sc_], 1).T
    return (np.ascontiguousarray(np.concatenate([cosT, cosT], 0), dtype=np.float32),
            np.ascontiguousarray(np.concatenate([sinT, sinT], 0), dtype=np.float32))


def consts():
    ones = np.ones((128, 128), np.float32)
    blk = np.zeros((128, 128), np.float32)
    blk[:64, :64] = 1
    blk[64:, 64:] = 1
    perm = np.zeros((128, 128), np.float32)
    for m in range(128):
        d = m % 64
        partner = d + 16 if (d % 32) < 16 else d - 16
        perm[(m // 64) * 64 + partner, m] = 1
    return np.stack([ones, blk, perm], 0)


def a_bias(table, r):
    out = np.empty((3, 128, 256), np.float32)
    i = np.arange(128)[:, None]
    j = np.arange(128)[None, :]
    for g, rate in enumerate(A_RATES):
        for t, off in enumerate((-64, 64)):
            rel = i - j + off
            b = table[t5_bucket_np(rel * rate), g * 4 + r]
            out[g, :, t * 128:(t + 1) * 128] = np.where(np.abs(rel) <= 64, b, NEG)
    return out


def b_bias(rpb, r):
    out = np.full((2, NCLS, 128, 640), NEG, np.float32)
    reps = {0: 2, 1: 0, 2: 1, 3: 62, 4: 63}
    ki = np.arange(128)[:, None]
    qj = np.arange(128)[None, :]
    for cls, n in reps.items():
        kts, c2 = b_tiles(n)
        assert c2 == cls
        qrow = 2 * n + qj // 64
        qcol = qj % 64
        rs = np.clip(qrow - 4, 0, 120)
        c0 = np.clip(qcol - 8, 0, 48)
        for idx, kt in enumerate(kts):
            krow = 2 * kt + ki // 64
            kcol = ki % 64
            ok = (krow >= rs) & (krow < rs + 8) & (kcol >= c0) & (kcol < c0 + 16)
            dr = np.clip(krow - qrow + 7, 0, 14)
            dc = np.clip(kcol - qcol + 15, 0, 30)
            for hi in range(2):
                b = rpb[2 * r + hi][dr, dc]
                out[hi, cls, :, idx * 128:(idx + 1) * 128] = np.where(ok, b, NEG)
    return out


def b_bias2(rpb, rp, r):
    out = np.full((2, NCLS, 128, 768), NEG, np.float32)
    reps = {0: 0, 1: 1, 2: 2, 3: 14, 4: 15}
    ki = np.arange(128)[:, None]
    qj = np.arange(128)[None, :]
    for cls, nl in reps.items():
        kts, c2 = b2_tiles(nl)
        assert c2 == cls
        n = 16 * r + nl
        qrow = 2 * n + qj // 64
        qcol = qj % 64
        rs = np.clip(qrow - 4, 0, 120)
        c0 = np.clip(qcol - 8, 0, 48)
        for idx, kt in enumerate(kts):
            krow = 32 * r - 4 + 2 * kt + ki // 64
            kcol = ki % 64
            ok = (krow >= 0) & (krow < 128) & (krow >= rs) & (krow < rs + 8) & (kcol >= c0) & (kcol < c0 + 16)
            dr = np.clip(krow - qrow + 7, 0, 14)
            dc = np.clip(kcol - qcol + 15, 0, 30)
            for hi in range(2):
                b = rpb[2 * rp + hi][dr, dc]
                out[hi, cls, :, idx * 128:(idx + 1) * 128] = np.where(ok, b, NEG)
    return out


def v_aug_tiles(v):
    n = v.shape[0] // 128
    va = np.concatenate([v, np.ones((v.shape[0], 1), v.dtype)], 1).reshape(n, 128, 65)
    return np.ascontiguousarray(va.transpose(1, 0, 2))


def a_v_tiles(v):
    out = np.zeros((3, 128, 80, 65), v[0].dtype)
    for g, rate in enumerate(A_RATES):
        L = SEQ // rate
        va = np.concatenate([v[g], np.ones((SEQ, 1), v[g].dtype)], 1)
        sub = va.reshape(L, rate, 65).transpose(1, 0, 2)
        pad = np.zeros((rate, L + 128, 65), v[g].dtype)
        pad[:, 64:64 + L] = sub
        tiles = pad.reshape(rate * (L // 128 + 1), 128, 65)
        out[g, :, :tiles.shape[0]] = tiles.transpose(1, 0, 2)
    return out


_NC = []


def get_nc():
    if not _NC:
        _NC.append(build_program())
    return _NC[0]


def prep_weights(p):
    m = {}
    wqk, wv, wg, pw, wo, wup, wdn = [], [], [], [], [], [], []
    for l in range(2):
        w_in = p["w_in"][l]
        wqk.append(chunk_major(w_in, QK_COLS, 128))
        vcols = np.concatenate([w_in[:, 1536:2304], w_in[:, 3328:3840], w_in[:, 4480:4608],
                                np.zeros((DM, 128), np.float32)], 1)
        wv.append(chunk_major(vcols, [0, 512, 1024], 512))
        wg.append(np.stack([chunk_major(w_in, [4608 + br * 1024 + dc * 128 for dc in range(8)], 128)
                            for br in range(3)], 2))
        pcat = np.concatenate([p["w_br_a"][l], p["w_br_b"][l], p["w_br_c"][l]], 0)
        pw.append(chunk_major(pcat, [dc * 128 for dc in range(8)], 128))
        wo.append(chunk_major(p["w_o"][l], [oc * 128 for oc in range(8)], 128))
        w_up = p["w_up"][l]
        wup.append(np.stack([chunk_major(w_up, [ab * 2816 + fc * 128 for fc in range(22)], 128)
                             for ab in range(2)], 2))
        wdn.append(chunk_major(p["w_down"][l], [oc * 128 for oc in range(8)], 128))
    m["wqk"] = np.ascontiguousarray(np.stack(wqk, 0))
    m["wv"] = np.ascontiguousarray(np.stack(wv, 0))
    m["wg"] = np.ascontiguousarray(np.stack(wg, 0))
    m["pw"] = np.ascontiguousarray(np.stack(pw, 0))
    m["wo"] = np.ascontiguousarray(np.stack(wo, 0))
    m["wup"] = np.ascontiguousarray(np.stack(wup, 0))
    m["wdn"] = np.ascontiguousarray(np.stack(wdn, 0))
    m["g1c"] = np.ascontiguousarray(p["norm1"].reshape(2, 8, 128).transpose(2, 0, 1))
    m["g2c"] = np.ascontiguousarray(p["norm2"].reshape(2, 8, 128).transpose(2, 0, 1))
    gq = np.stack([np.stack([np.tile(p["qk_gain"][l][gi], 2) for gi in QK_GAIN], 1) for l in range(2)], 1)
    m["gq"] = np.ascontiguousarray(gq)
    cosT, sinT = rope_tables()
    m["cosT"], m["sinT"] = cosT, sinT
    m["cst"] = consts()
    m["abias"] = np.ascontiguousarray(np.stack([a_bias(p["rel_bias_table"], r) for r in range(4)], 0))
    m["bbias"] = np.ascontiguousarray(np.stack(
        [np.stack([b_bias(p["nat_rpb"][l], r) for r in range(4)], 0) for l in range(2)], 0))
    sel = np.zeros((65, 64), np.float32)
    sel[64, :] = 1
    m["sel"] = sel
    return m


def kernel(x, rel_bias_table, norm1, w_in, qk_gain, nat_rpb, w_br_a, w_br_b, w_br_c, w_o, norm2, w_up, w_down):
    p = dict(rel_bias_table=np.asarray(rel_bias_table, np.float32), norm1=np.asarray(norm1, np.float32),
             w_in=np.asarray(w_in, np.float32), qk_gain=np.asarray(qk_gain, np.float32),
             nat_rpb=np.asarray(nat_rpb, np.float32), w_br_a=np.asarray(w_br_a, np.float32),
             w_br_b=np.asarray(w_br_b, np.float32), w_br_c=np.asarray(w_br_c, np.float32),
             w_o=np.asarray(w_o, np.float32), norm2=np.asarray(norm2, np.float32),
             w_up=np.asarray(w_up, np.float32), w_down=np.asarray(w_down, np.float32))
    x = np.asarray(x, np.float32)
    wm = prep_weights(p)
    xTb = [np.ascontiguousarray(x[b].T) for b in range(2)]
    bb2 = [np.ascontiguousarray(np.stack([b_bias2(p["nat_rpb"][1], rp, r) for rp in range(4)], 0)) for r in range(4)]
    in_maps = []
    for c in range(NCORES):
        m = dict(wm)
        m["xT"] = xTb[c // 4]
        m["bbias2"] = bb2[c % 4]
        in_maps.append(m)
    res = run_bass_kernel_spmd(get_nc(), in_maps, core_ids=list(range(NCORES))).results
    out = np.empty((2, SEQ, DM), np.float32)
    for c in range(NCORES):
        out[c // 4, (c % 4) * NT:(c % 4 + 1) * NT, :] = res[c]["yT"].T
    return out
```

```python
import numpy as np
import ml_dtypes
from contextlib import ExitStack
import concourse.bass as bass
import concourse.mybir as mybir
from concourse.bass_utils import run_bass_kernel_spmd

F32 = mybir.dt.float32
BF16 = mybir.dt.bfloat16
AF = mybir.ActivationFunctionType
ALU = mybir.AluOpType
NPBF = ml_dtypes.bfloat16

NCORES = 8
SEQ = 8192
DM = 1024
NT = 2048
EPS = 1e-6
NEG = -1e30
A_RATES = (1, 4, 16)
NDS = 12
ARENA = 46592
PADT = 1024


class Prog:
    COMP = ("pe", "act", "dve")
    DMAQ = ("sp", "pool")

    def __init__(self):
        self.nc = bass.Bass("TRN2", target_bir_lowering=False)
        self.es = ExitStack()
        self.ops = {e: [] for e in self.COMP + self.DMAQ}
        self.ncomp = {e: 0 for e in self.COMP}
        self.ndma = {q: 0 for q in self.DMAQ}
        self.lastw = {}
        self.rd = {}
        self.waited = {}
        self.semkeys = set()
        self.big = self.es.enter_context(self.nc.sbuf_tensor("arena", [128, ARENA], F32))
        self.aoff = 0
        self.psall = self.es.enter_context(self.nc.psum_tensor("psum_all", [128, 4096], F32))
        self.bank = [self.psall[:, 512 * i:512 * (i + 1)] for i in range(8)]
        self._own = {}
        self.eps_t = self.es.enter_context(self.nc.sbuf_tensor("eps_c", [128, 1], F32))
        self.eps = self.eps_t[:]

    def arena_reset(self):
        self.aoff = 0

    def _carve(self, shape, words):
        words = (words + 15) // 16 * 16
        assert self.aoff + words <= ARENA, ("arena overflow", self.aoff, words)
        ap = self.big[0:shape[0], self.aoff:self.aoff + words]
        self.aoff += words
        return ap

    @staticmethod
    def _shape(ap, shape):
        if len(shape) == 2:
            return ap
        if len(shape) == 3:
            return ap.rearrange("p (a b) -> p a b", a=shape[1])
        return ap.rearrange("p (a b c) -> p a b c", a=shape[1], b=shape[2])

    def a32(self, shape):
        n = int(np.prod(shape[1:]))
        return self._shape(self._carve(shape, n)[:, 0:n], shape)

    def a16(self, shape):
        n = int(np.prod(shape[1:]))
        return self._shape(self._carve(shape, (n + 1) // 2).bitcast(BF16)[:, 0:n], shape)

    def own(self, e):
        k = id(e)
        if k not in self._own:
            self._own[k] = (e.partition_id() % 4) * NT
        return self._own[k]

    def barrier(self):
        tg = {}
        for en in self.COMP:
            if self.ncomp[en] > 0:
                tg[("c", en)] = self.ncomp[en]
        for q in self.DMAQ:
            n = self.ndma[q]
            for j in range(min(NDS, n)):
                tg[("d", q, j)] = 16 * ((n - j + NDS - 1) // NDS)
        for en in self.COMP + self.DMAQ:
            waits = []
            for sk, v in tg.items():
                if self.waited.get((en, sk), 0) < v:
                    self.waited[(en, sk)] = v
                    waits.append((sk, v))
                    self.semkeys.add(sk)
            self.ops[en].append((waits, None, None, 0))
        self.lastw.clear()
        self.rd.clear()

    def dram(self, name, shape, dt, out=False):
        kind = "ExternalOutput" if out else "ExternalInput"
        return self.nc.dram_tensor(name, list(shape), dt, kind=kind).ap()

    def sb(self, name, shape, dt):
        return self.es.enter_context(self.nc.sbuf_tensor(name, list(shape), dt))

    def scratch(self, name, shape, dt):
        return self.nc.dram_tensor(name, list(shape), dt).ap()

    def op(self, eng, fn, reads=(), writes=(), extra=()):
        is_dma = eng in self.DMAQ
        deps = [(t, "raw") for t in extra if t is not None]
        for k in reads:
            t = self.lastw.get(k)
            if t is not None:
                deps.append((t, "raw"))
        for k in writes:
            t = self.lastw.get(k)
            if t is not None:
                deps.append((t, "waw"))
            for t in self.rd.get(k, ()):
                deps.append((t, "war"))
        need = {}
        for (semkey, val, teng, tdma), kind in deps:
            if (not tdma) and (not is_dma) and teng == eng and kind != "raw":
                continue
            need[semkey] = max(need.get(semkey, 0), val)
        if is_dma:
            n = self.ndma[eng]
            self.ndma[eng] += 1
            j = n % NDS
            semkey = ("d", eng, j)
            val = 16 * (n // NDS + 1)
            if n >= NDS:
                need[semkey] = max(need.get(semkey, 0), val - 16)
            inc = 16
        else:
            self.ncomp[eng] += 1
            semkey = ("c", eng)
            val = self.ncomp[eng]
            inc = 1
        waits = []
        for sk, v in need.items():
            if self.waited.get((eng, sk), 0) >= v:
                continue
            self.waited[(eng, sk)] = v
            waits.append((sk, v))
            self.semkeys.add(sk)
        self.semkeys.add(semkey)
        self.ops[eng].append((waits, fn, semkey, inc))
        tok = (semkey, val, eng, is_dma)
        for k in writes:
            self.lastw[k] = tok
            self.rd[k] = []
        for k in reads:
            lst = self.rd.setdefault(k, [])
            if not is_dma:
                lst[:] = [t for t in lst if not (t[2] == eng and not t[3])]
            lst.append(tok)
        return tok

    def emit(self):
        nc = self.nc
        sems = {}
        for i, sk in enumerate(sorted(self.semkeys, key=str)):
            sems[sk] = self.es.enter_context(nc.semaphore("s%d" % i))
        ops = self.ops
        ndma = self.ndma

        def mk(eng):
            def body(e):
                for waits, fn, semkey, inc in ops[eng]:
                    for sk, v in waits:
                        e.wait_ge(sems[sk], v)
                    if fn is not None:
                        fn(e).then_inc(sems[semkey], inc)
                if eng in ndma:
                    n = ndma[eng]
                    for j in range(min(NDS, n)):
                        cnt = (n - j + NDS - 1) // NDS
                        e.wait_ge(sems[("d", eng, j)], 16 * cnt)
            return body

        with nc.Block() as block:
            block.tensor(mk("pe"))
            block.scalar(mk("act"))
            block.vector(mk("dve"))
            block.gpsimd(mk("pool"))
            block.sync(mk("sp"))
        self.es.close()
        return nc


def mm(out, lhsT, rhs, start, stop):
    return lambda e: e.matmul(out, lhsT=lhsT, rhs=rhs, start=start, stop=stop)


def dma(out, in_):
    return lambda e: e.dma_start(out=out, in_=in_)


def dmaf(out_fn, in_fn):
    return lambda e: e.dma_start(out=out_fn(e), in_=in_fn(e))


def memset(out, val):
    return lambda e: e.memset(out, val)


def act(out, in_, func, scale=None, bias=None):
    kw = {}
    if scale is not None:
        kw["scale"] = scale
    if bias is not None:
        kw["bias"] = bias
    return lambda e: e.activation(out=out, in_=in_, func=func, **kw)


def stt(out, in0, scalar, in1, op0, op1):
    return lambda e: e.scalar_tensor_tensor(out=out, in0=in0, scalar=scalar, in1=in1, op0=op0, op1=op1)


def tt(out, in0, in1, op):
    return lambda e: e.tensor_tensor(out=out, in0=in0, in1=in1, op=op)


def tcopy(out, in_):
    return lambda e: e.tensor_copy(out=out, in_=in_)


def acopy(out, in_):
    return lambda e: e.copy(out=out, in_=in_)


def recip(out, in_):
    return lambda e: e.reciprocal(out=out, in_=in_)


def tscal(out, in0, s1, s2, op0, op1):
    return lambda e: e.tensor_scalar(out=out, in0=in0, scalar1=s1, scalar2=s2, op0=op0, op1=op1)


def emit_rstd(P, ss_ps, sskey, tmp, tmpkey, inv_n):
    P.op("act", act(tmp, ss_ps, AF.Ln, scale=inv_n, bias=P.eps), reads=[sskey], writes=[tmpkey])
    P.op("act", act(tmp, tmp, AF.Exp, scale=-0.5), reads=[tmpkey], writes=[tmpkey])


def emit_xnorm(P, xT, xkey, hT, hkey, gcol, ones, sq, ss_ps, rstd, ntc):
    for tc in range(ntc):
        ts = slice(tc * 512, (tc + 1) * 512)
        for k in range(8):
            s = k % 2
            P.op("act", act(sq[s][:], xT[:, k, ts], AF.Square), reads=[(xkey, tc)], writes=[("sq", s)])
            P.op("pe", mm(ss_ps, ones, sq[s][:], k == 0, k == 7), reads=[("sq", s), "cst"], writes=["ss_ps"])
        emit_rstd(P, ss_ps, "ss_ps", rstd, "rstd", 1.0 / DM)
        for k in range(8):
            P.op("dve", stt(hT[:, k, ts], xT[:, k, ts], gcol[:, k:k + 1], rstd, ALU.mult, ALU.mult),
                 reads=[(xkey, tc), "rstd", "gcol"], writes=[(hkey, tc)])


NQK = 25
ROPE_CH = (20, 21, 22, 23, 24)
NVH = 22
VW = NVH * 65
NCLS = 5
TB = 1024
Q_CHUNKS = [0, 1, 2, 3, 4, 5, 12, 13, 14, 15, 20, 21, 22, 23]
K_CHUNKS = [6, 7, 8, 9, 10, 11, 16, 17, 18, 19, 24]
OCH = ((0, 2), (2, 6), (6, 10))


class Holder:
    pass


def setup(P):
    D = Holder()
    D.xT = P.dram("xT", [DM, SEQ], F32)
    D.g1 = P.dram("g1c", [128, 2, 8], F32)
    D.g2 = P.dram("g2c", [128, 2, 8], F32)
    D.gq = P.dram("gq", [128, 2, NQK], F32)
    D.wqk = P.dram("wqk", [2, NQK, 128, 8, 128], F32)
    D.wv = P.dram("wv", [2, 3, 128, 8, 512], F32)
    D.cos = P.dram("cosT", [128, SEQ], F32)
    D.sin = P.dram("sinT", [128, SEQ], F32)
    D.cstd = P.dram("cst", [3, 128, 128], F32)
    D.abias = P.dram("abias", [4, 3, 128, 256], F32)
    D.bbias = P.dram("bbias", [2, 4, 2, NCLS, 128, 640], F32)
    D.seld = P.dram("sel", [65, 64], F32)
    D.wg = P.dram("wg", [2, 8, 128, 3, 8, 128], F32)
    D.pw = P.dram("pw", [2, 8, 128, 10, 128], F32)
    D.wo = P.dram("wo", [2, 8, 128, 8, 128], F32)
    D.wup = P.dram("wup", [2, 22, 128, 2, 8, 128], F32)
    D.wdn = P.dram("wdn", [2, 8, 128, 22, 128], F32)
    D.bbias2 = P.dram("bbias2", [4, 2, NCLS, 128, 768], F32)
    D.y = P.dram("yT", [DM, NT], F32, out=True)
    D.q2_scr = P.scratch("q2_scr", [len(Q_CHUNKS) * 128, NT], BF16)
    D.o2_scr = P.scratch("o2_scr", [1280, NT], BF16)
    D.xo_scr = P.scratch("xo_scr", [DM, NT], F32)
    D.coso = P.scratch("cos_own", [128, NT], F32)
    D.sino = P.scratch("sin_own", [128, NT], F32)
    D.kwA = P.scratch("kwA_scr", [768, NT + 2 * PADT], BF16)
    D.kwB = P.scratch("kwB_scr", [512, 2560], BF16)
    D.vw = P.scratch("vw_scr", [NT + 2 * PADT + 16, VW], BF16)
    D.qk_scr = P.scratch("qk_scr", [NQK * 128, SEQ + 2 * PADT], BF16)
    D.v_scr = P.scratch("v_scr", [SEQ + 2 * PADT + 16, VW], BF16)
    D.o_scr = P.scratch("o_scr", [1280, SEQ], BF16)
    D.x_scr = P.scratch("x_scr", [DM, SEQ], F32)
    D.cst = P.sb("cst_sb", [128, 3, 128], BF16)
    D.sel = P.sb("sel_sb", [65, 64], F32)
    D.g1s = P.sb("g1_sb", [128, 2, 8], F32)
    D.g2s = P.sb("g2_sb", [128, 2, 8], F32)
    D.gqs = P.sb("gq_sb", [128, 2, NQK], F32)
    return D


def phase_init(P, D):
    P.arena_reset()
    z = P.a16([128, 8, VW])
    P.op("dve", memset(z, 0.0), writes=["z"])
    P.op("dve", memset(P.eps, EPS), writes=["eps"])
    P.op("pool", dma(D.cst[:], D.cstd.rearrange("a p c -> p a c")), writes=["cst"])
    P.op("sp", dma(D.sel[:], D.seld[:, :]), writes=["sel"])
    P.op("sp", dma(D.g1s[:], D.g1[:, :, :]), writes=["g1"])
    P.op("sp", dma(D.g2s[:], D.g2[:, :, :]), writes=["g2"])
    P.op("sp", dma(D.gqs[:], D.gq[:, :, :]), writes=["gq"])
    for r0 in (0, PADT + SEQ):
        P.op("sp", dma(D.v_scr[r0:r0 + 1024, :].rearrange("(t p) c -> p t c", p=128), z), reads=["z"])
    P.op("sp", dma(D.v_scr[SEQ + 2 * PADT:SEQ + 2 * PADT + 16, :], z[0:16, 0, :]), reads=["z"])
    zf = z.rearrange("p t c -> p (t c)")
    for cc in list(range(6, 12)) + list(range(16, 20)) + [24]:
        for c0 in (0, PADT + SEQ):
            P.op("sp", dma(D.qk_scr[cc * 128:(cc + 1) * 128, c0:c0 + PADT], zf[:, 0:PADT]), reads=["z"])


def phase_L1(P, D, l, x_src, blocks, chunks=None, do_v=True, NB=1024, own=False):
    P.barrier()
    P.arena_reset()
    chunks = list(range(NQK)) if chunks is None else list(chunks)
    ntc = NB // 512
    xT = P.a32([128, 8, NB])
    cosT = P.a32([128, NB])
    sinT = P.a32([128, NB])
    rstd = P.a32([128, 512])
    r32 = [P.a32([128, 512]) for _ in range(2)]
    t1 = [P.a32([128, 512]) for _ in range(2)]
    t2 = [P.a32([128, 512]) for _ in range(2)]
    hT = P.a16([128, 8, NB])
    wv = P.a16([128, 3, 8, 512])
    NW = 3
    w = [P.a16([128, 8, 128]) for _ in range(NW)]
    sq = [P.a16([128, 512]) for _ in range(2)]
    qnb = [P.a16([128, 512]) for _ in range(2)]
    ob = [P.a16([128, NB]) for _ in range(2)]
    vb = [P.a16([128, NVH, 65]) for _ in range(2)]
    ss_ps = P.bank[0][:]
    z_ps = [P.bank[1], P.bank[2], P.bank[6], P.bank[7]]
    st_ps = [P.bank[3], P.bank[4]]
    pm_ps = P.bank[5]
    v_ps = [P.bank[6], P.bank[7]]
    NZ = 4
    ones, blk, perm = D.cst[:, 0, :], D.cst[:, 1, :], D.cst[:, 2, :]
    g1 = D.g1s[:, l, :]
    gq = D.gqs[:, l, :]

    def L1_block(c0):
        for tc in range(ntc):
            P.op("sp", dma(xT[:, :, tc * 512:(tc + 1) * 512], xv[:, :, c0 + tc * 512:c0 + (tc + 1) * 512]),
                 writes=[("x", tc)])
        P.op("sp", dma(cosT, cos_src[:, c0:c0 + NB]), writes=["cos"])
        P.op("sp", dma(sinT, sin_src[:, c0:c0 + NB]), writes=["sin"])

        emit_xnorm(P, xT, "x", hT, "h", g1, ones, sq, ss_ps, rstd, ntc)

        units = [(cc, tc) for cc in chunks for tc in range(ntc)]

        def stageA(u):
            cc, tc = units[u]
            ws = (u // ntc) % NW
            if tc == 0:
                P.op("pool", dma(w[ws], D.wqk[l, cc]), writes=[("w", ws)])
            zb = u % NZ
            for k in range(8):
                P.op("pe", mm(z_ps[zb][:], w[ws][:, k, :], hT[:, k, tc * 512:(tc + 1) * 512], k == 0, k == 7),
                     reads=[("w", ws), ("h", tc)], writes=[("z", zb)])

        def stageB1(u):
            zb = u % NZ
            sb_ = u % 2
            P.op("act", act(sq[sb_], z_ps[zb][:], AF.Square), reads=[("z", zb)], writes=[("sq", sb_)])
            P.op("pe", mm(st_ps[sb_][:], blk, sq[sb_], True, True), reads=[("sq", sb_), "cst"], writes=[("st", sb_)])

        def stageB2(u):
            cc, tc = units[u]
            zb = u % NZ
            sb_ = u % 2
            ts = slice(tc * 512, (tc + 1) * 512)
            osl = (u // ntc) % 2
            emit_rstd(P, st_ps[sb_][:], ("st", sb_), r32[sb_], ("r32", sb_), 1.0 / 64)
            rope = cc in ROPE_CH
            dst = qnb[sb_] if rope else ob[osl][:, ts]
            dkey = ("qnb", sb_) if rope else ("ob", osl)
            P.op("dve", stt(dst, z_ps[zb][:], gq[:, cc:cc + 1], r32[sb_], ALU.mult, ALU.mult),
                 reads=[("z", zb), ("r32", sb_), "gq"], writes=[dkey])
            if rope:
                P.op("pe", mm(pm_ps[:], perm, qnb[sb_], True, True), reads=[("qnb", sb_), "cst"], writes=["pm"])
                P.op("dve", tt(t1[sb_], qnb[sb_], cosT[:, ts], ALU.mult), reads=[("qnb", sb_), "cos"], writes=[("t1", sb_)])
                P.op("dve", tt(t2[sb_], pm_ps[:], sinT[:, ts], ALU.mult), reads=["pm", "sin"], writes=[("t2", sb_)])
                P.op("dve", tt(ob[osl][:, ts], t1[sb_], t2[sb_], ALU.add),
                     reads=[("t1", sb_), ("t2", sb_)], writes=[("ob", osl)])
            if tc == ntc - 1:
                if own:
                    qi = Q_CHUNKS.index(cc)
                    P.op("sp", dma(D.q2_scr[qi * 128:(qi + 1) * 128, c0:c0 + NB], ob[osl]), reads=[("ob", osl)])
                else:
                    P.op("sp", dma(D.qk_scr[cc * 128:(cc + 1) * 128, PADT + c0:PADT + c0 + NB], ob[osl]),
                         reads=[("ob", osl)])

        n = len(units)
        for i in range(n + 2):
            if i < n:
                stageA(i)
            if 1 <= i <= n:
                stageB1(i - 1)
            if i >= 2:
                stageB2(i - 2)

        if do_v:
            for tk in range(NB // 128):
                vs = tk % 2
                for g in range(3):
                    nh = 8 if g < 2 else 6
                    pb = (tk * 3 + g) % 2
                    for k in range(8):
                        P.op("pe", mm(v_ps[pb][:, 0:nh * 64], hT[:, k, tk * 128:(tk + 1) * 128], wv[:, g, k, 0:nh * 64],
                                      k == 0, k == 7), reads=[("h", tk // 4), ("wv", g)], writes=[("z", 2 + pb)])
                    src = v_ps[pb][:, 0:nh * 64].rearrange("p (h c) -> p h c", c=64)
                    dst = vb[vs][:, 8 * g:8 * g + nh, 0:64]
                    if g % 2 == 0:
                        P.op("act", acopy(dst, src), reads=[("z", 2 + pb)], writes=[("vb", vs)])
                    else:
                        P.op("dve", tcopy(dst, src), reads=[("z", 2 + pb)], writes=[("vb", vs)])
                r0 = PADT + c0 + tk * 128
                P.op("sp", dma(D.v_scr[r0:r0 + 128, :], vb[vs].rearrange("p h c -> p (h c)")), reads=[("vb", vs)])

    xv = x_src.rearrange("(k p) t -> p k t", p=128)
    cos_src, sin_src = (D.coso, D.sino) if own else (D.cos, D.sin)
    if do_v:
        for g in range(3):
            P.op("pool", dma(wv[:, g], D.wv[l, g]), writes=[("wv", g)])
        for i in range(2):
            P.op("dve", memset(vb[i][:, :, 64:65], 1.0), writes=[("vb", i)])
    ucount = [0]
    for c0 in blocks:
        L1_block(c0)


def b_tiles(n):
    if n <= 1:
        return [0, 1, 2, 3], 1 + n
    if n >= 62:
        return [60, 61, 62, 63], 3 + (n - 62)
    return [n - 2, n - 1, n, n + 1, n + 2], 0


def b2_tiles(nl):
    if nl == 0:
        return [0, 1, 2, 3, 4, 5], 0
    if nl == 1:
        return [1, 2, 3, 4, 5], 1
    if nl == 14:
        return [14, 15, 16, 17, 18], 3
    if nl == 15:
        return [14, 15, 16, 17, 18, 19], 4
    return [nl, nl + 1, nl + 2, nl + 3, nl + 4], 2


def norm_out(P, stg_ap, stgkey, osb_ap, osbkey, sel, den_ps, rden, out_ap, use_act):
    P.op("pe", mm(den_ps[0:64, :], sel, stg_ap[0:65, :], True, True), reads=[stgkey, "sel"], writes=["den"])
    if use_act:
        P.op("act", act(rden[0:64, :], den_ps[0:64, :], AF.Ln), reads=["den"], writes=["rden"])
        P.op("act", act(rden[0:64, :], rden[0:64, :], AF.Exp, scale=-1.0), reads=["rden"], writes=["rden"])
    else:
        P.op("dve", recip(rden[0:64, :], den_ps[0:64, :]), reads=["den"], writes=["rden"])
    P.op("dve", tt(osb_ap, stg_ap[0:64, :], rden[0:64, :], ALU.mult), reads=[stgkey, "rden"], writes=[osbkey])
    P.op("sp", dma(out_ap, osb_ap), reads=[osbkey])


def vload(P, v_sb, slot, base, src, nt):
    t = 0
    while t < nt:
        a = base + t
        n = min(nt - t, 16 - a % 16)
        P.op("sp", dma(v_sb[:, slot, a:a + n, :], src[:, t:t + n, :]), writes=[("v", slot, a // 16)])
        t += n


def phase_att(P, D, l, rp, own):
    P.barrier()
    P.arena_reset()
    NQ = NT if own else SEQ
    KW = NQ + 2 * PADT
    BW = 768 if own else 640
    q_sb = P.a16([128, NQ])
    k_sb = P.a16([128, SEQ + 2 * PADT]) if not own else P.a16([128, SEQ])
    v_sb = P.a16([128, 2, 80, 65])
    pT = [P.a16([128, 768]) for _ in range(4)]
    pTC = [P.a16([128, 1024]) for _ in range(3)]
    pTA = [P.a16([128, 256]) for _ in range(8)]
    osb = [P.a16([128, 512]) for _ in range(2)]
    abias = P.a32([128, 3, 256])
    bbias = P.a32([128, 2, NCLS, BW])
    oacc = P.a32([65, NQ])
    sc = [P.a32([128, 768]) for _ in range(2)]
    scA = [P.a32([128, 256]) for _ in range(8)]
    stg = [P.a32([65, 512]) for _ in range(2)]
    rden = P.a32([64, 512])
    s_ps = [P.bank[i] for i in range(4)]
    o_ps = [P.bank[4], P.bank[5]]
    den_ps = P.bank[6]
    sel = D.sel[:]
    qk = D.qk_scr
    q2 = D.q2_scr
    v3 = D.v_scr.rearrange("t (h c) -> t h c", c=65)
    va3 = D.vw.rearrange("t (h c) -> t h c", c=65) if own else v3
    o_dst = D.o2_scr if own else D.o_scr

    P.op("sp", dma(abias, D.abias[rp].rearrange("g p c -> p g c")), writes=["abias"])
    for hi in range(2):
        bsrc = D.bbias2[rp, hi] if own else D.bbias[l, rp, hi]
        P.op("sp", dma(bbias[:, hi], bsrc.rearrange("c p x -> p c x")), writes=[("bbias", hi)])

    ro = 64 * (rp % 2)
    sreg = [s_ps[i][:, 0:256] for i in range(4)]
    oreg = [P.bank[4 + i][0:65, 0:128] for i in range(4)]
    for g, rate in enumerate(A_RATES):
        if g > 0:
            P.barrier()
        L_ = NQ // rate
        nqb = L_ // 128
        qrow = (2 * g + rp // 2) * 128 + ro
        if own:
            P.op("sp", dma(q_sb[0:64, :], q2[qrow:qrow + 64, :]), writes=["q"])
            P.op("sp", dma(k_sb[0:64, 0:KW], D.kwA[qrow:qrow + 64, :]), writes=["k"])
        else:
            krow = (6 + 2 * g + rp // 2) * 128 + ro
            P.op("sp", dma(q_sb[0:64, :], qk[qrow:qrow + 64, PADT:PADT + SEQ]), writes=["q"])
            P.op("sp", dma(k_sb[0:64, 0:KW], qk[krow:krow + 64, :]), writes=["k"])
        head = g * 4 + rp
        M = L_ + 128
        for res in range(rate):
            start = PADT - 64 * rate + res
            src = va3[start:start + rate * M].rearrange("(m r) h c -> m r h c", r=rate)[:, 0, head, :]
            src = src.rearrange("(t p) c -> p t c", p=128)
            vload(P, v_sb, 0, res * (nqb + 1), src, nqb + 1)
        qv = q_sb[0:64, :].rearrange("p (l r) -> p r l", r=rate)
        kv = k_sb[0:64, 0:KW].rearrange("p (l r) -> p r l", r=rate)
        ov = oacc.rearrange("p (l r) -> p r l", r=rate)
        koff = PADT // rate
        units = [(res, qb) for res in range(rate) for qb in range(nqb)]

        def A1(u):
            res, qb = units[u]
            i4 = u % 4
            qs = qv[:, res, 128 * qb:128 * qb + 128]
            l0 = koff + 128 * qb - 64
            P.op("pe", mm(sreg[i4][:, 0:128], kv[:, res, l0:l0 + 128], qs, True, True),
                 reads=["q", "k"], writes=[("s", i4)])
            P.op("pe", mm(sreg[i4][:, 128:256], kv[:, res, l0 + 128:l0 + 256], qs, True, True),
                 reads=["q", "k"], writes=[("s", i4)])

        def A2(u):
            i4 = u % 4
            i8 = u % 8
            P.op("dve", stt(scA[i8], sreg[i4], 0.125, abias[:, g, :], ALU.mult, ALU.add),
                 reads=[("s", i4), "abias"], writes=[("scA", i8)])
            P.op("act", act(pTA[i8], scA[i8], AF.Exp), reads=[("scA", i8)], writes=[("pTA", i8)])

        def A3(u):
            res, qb = units[u]
            i4 = u % 4
            i8 = u % 8
            tA = res * (nqb + 1) + qb
            P.op("pe", mm(oreg[i4], v_sb[:, 0, tA, :], pTA[i8][:, 0:128], True, False),
                 reads=[("pTA", i8), ("v", 0, tA // 16)], writes=[("oA", i4)])
            P.op("pe", mm(oreg[i4], v_sb[:, 0, tA + 1, :], pTA[i8][:, 128:256], False, True),
                 reads=[("pTA", i8), ("v", 0, (tA + 1) // 16)], writes=[("oA", i4)])
            dst = ov[:, res, 128 * qb:128 * qb + 128]
            if g == 0:
                P.op("dve", tcopy(dst, oreg[i4]), reads=[("oA", i4)], writes=["oacc"])
            else:
                P.op("dve", tt(dst, oreg[i4], dst, ALU.add), reads=[("oA", i4), "oacc"], writes=["oacc"])

        n = len(units)
        for i in range(n + 3):
            if i < n:
                A1(i)
            if 1 <= i < n + 1:
                A2(i - 1)
            if 3 <= i:
                A3(i - 3)
    P.barrier()
    for c in range(NQ // 512):
        cs = slice(c * 512, (c + 1) * 512)
        o_ = c % 2
        norm_out(P, oacc[:, cs], "oacc", osb[o_][0:64, :], ("osb", o_), sel, den_ps, rden,
                 o_dst[rp * 64:rp * 64 + 64, cs], use_act=True)

    P.barrier()
    if own:
        P.op("sp", dma(q_sb, q2[(6 + rp) * 128:(7 + rp) * 128, :]), writes=["q"])
        P.op("sp", dma(k_sb[:, 0:2560], D.kwB[rp * 128:(rp + 1) * 128, :]), writes=["k"])
        for hi in range(2):
            src = va3[PADT - 256:PADT - 256 + 2560, 12 + 2 * rp + hi, :].rearrange("(t p) c -> p t c", p=128)
            vload(P, v_sb, hi, 0, src, 20)
        tiles_fn = b2_tiles
    else:
        P.op("sp", dma(q_sb, qk[(12 + rp) * 128:(13 + rp) * 128, PADT:PADT + SEQ]), writes=["q"])
        P.op("sp", dma(k_sb[:, 0:SEQ], qk[(16 + rp) * 128:(17 + rp) * 128, PADT:PADT + SEQ]), writes=["k"])
        for hi in range(2):
            src = v3[PADT:PADT + SEQ, 12 + 2 * rp + hi, :].rearrange("(t p) c -> p t c", p=128)
            vload(P, v_sb, hi, 0, src, 64)
        tiles_fn = b_tiles
    unitsB = [(c, j, hi) for c in range(NQ // 512) for j in range(4) for hi in range(2)]
    pending = []

    def B1(u):
        c, j, hi = unitsB[u]
        n_ = 4 * c + j
        kts, cls = tiles_fn(n_)
        hp = slice(64 * hi, 64 * hi + 64)
        qs = q_sb[hp, 128 * n_:128 * n_ + 128]
        sa, sb2 = s_ps[2 * (u % 2)], s_ps[2 * (u % 2) + 1]
        for idx, kt in enumerate(kts):
            dst = sa[:, idx * 128:(idx + 1) * 128] if idx < 4 else sb2[:, (idx - 4) * 128:(idx - 3) * 128]
            P.op("pe", mm(dst, k_sb[hp, 128 * kt:128 * kt + 128], qs, True, True),
                 reads=["q", "k"], writes=[("s", 2 * (u % 2) + (0 if idx < 4 else 1))])

    def B2(u):
        c, j, hi = unitsB[u]
        kts, cls = tiles_fn(4 * c + j)
        b2 = u % 2
        p4 = u % 4
        sa, sb2 = s_ps[2 * b2], s_ps[2 * b2 + 1]
        nk = len(kts)
        P.op("dve", stt(sc[b2][:, 0:512], sa[:, :], 0.125, bbias[:, hi, cls, 0:512], ALU.mult, ALU.add),
             reads=[("s", 2 * b2), ("bbias", hi)], writes=[("sc", b2)])
        if nk > 4:
            w2 = (nk - 4) * 128
            P.op("dve", stt(sc[b2][:, 512:512 + w2], sb2[:, 0:w2], 0.125, bbias[:, hi, cls, 512:512 + w2],
                            ALU.mult, ALU.add), reads=[("s", 2 * b2 + 1), ("bbias", hi)], writes=[("sc", b2)])
        P.op("act", act(pT[p4][:, 0:nk * 128], sc[b2][:, 0:nk * 128], AF.Exp), reads=[("sc", b2)], writes=[("pT", p4)])

    def B3(u):
        for due, fn in list(pending):
            if due <= u:
                fn()
                pending.remove((due, fn))
        c, j, hi = unitsB[u]
        kts, cls = tiles_fn(4 * c + j)
        p4 = u % 4
        nk = len(kts)
        for idx, kt in enumerate(kts):
            P.op("pe", mm(o_ps[hi][0:65, j * 128:(j + 1) * 128], v_sb[:, hi, kt, :], pT[p4][:, idx * 128:(idx + 1) * 128],
                          idx == 0, idx == nk - 1), reads=[("pT", p4), ("v", hi, kt // 16)], writes=[("o", hi)])
        if j == 3:
            P.op("act", acopy(stg[hi], o_ps[hi][0:65, :]), reads=[("o", hi)], writes=[("stg", hi)])
            r0 = 256 + rp * 128 + 64 * hi
            pending.append((u + 3, (lambda hi=hi, r0=r0, c=c: norm_out(
                P, stg[hi], ("stg", hi), osb[hi][0:64, :], ("osb", hi), sel, den_ps, rden,
                o_dst[r0:r0 + 64, c * 512:(c + 1) * 512], use_act=True))))

    n = len(unitsB)
    for i in range(n + 2):
        if i < n:
            B1(i)
        if 1 <= i <= n:
            B2(i - 1)
        if i >= 2:
            B3(i - 2)
    for due, fn in pending:
        fn()
    pending.clear()

    P.barrier()
    if own:
        P.op("sp", dma(q_sb, q2[(10 + rp) * 128:(11 + rp) * 128, :]), writes=["q"])
    else:
        P.op("sp", dma(q_sb, qk[(20 + rp) * 128:(21 + rp) * 128, PADT:PADT + SEQ]), writes=["q"])
    kr2 = 24 * 128 + 64 * (rp // 2)
    for hi in range(2):
        P.op("sp", dma(k_sb[64 * hi:64 * hi + 64, 0:SEQ], qk[kr2:kr2 + 64, PADT:PADT + SEQ]), writes=[("k", hi)])
    src = v3[PADT:PADT + SEQ, 20 + rp // 2, :].rearrange("(t p) c -> p t c", p=128)
    vload(P, v_sb, 0, 0, src, 64)
    unitsC = [(c, kt) for c in range(NQ // 512) for kt in range(64)]

    def C1(u):
        c, kt = unitsC[u]
        for hi in range(2):
            hp = slice(64 * hi, 64 * hi + 64)
            sb_ = 2 * (u % 2) + hi
            P.op("pe", mm(s_ps[sb_][:, :], k_sb[hp, 128 * kt:128 * kt + 128], q_sb[hp, c * 512:(c + 1) * 512], True, True),
                 reads=["q", ("k", hi)], writes=[("s", sb_)])

    def C2(u):
        b2 = u % 2
        sb0 = 2 * b2
        P.op("act", act(pTC[u % 3], P.psall[:, 512 * sb0:512 * sb0 + 1024], AF.Exp, scale=0.125),
             reads=[("s", sb0), ("s", sb0 + 1)], writes=[("pTC", u % 3)])

    def C3(u):
        c, kt = unitsC[u]
        if kt == 6:
            for fn in pending:
                fn()
            pending.clear()
        p3 = u % 3
        for hi in range(2):
            P.op("pe", mm(o_ps[hi][0:65, :], v_sb[:, 0, kt, :], pTC[p3][:, 512 * hi:512 * (hi + 1)], kt == 0, kt == 63),
                 reads=[("pTC", p3), ("v", 0, kt // 16)], writes=[("o", hi)])
            if kt == 63:
                P.op("dve", tcopy(stg[hi], o_ps[hi][0:65, :]), reads=[("o", hi)], writes=[("stg", hi)])
                r0 = 768 + rp * 128 + 64 * hi
                pending.append((lambda hi=hi, r0=r0, c=c: norm_out(
                    P, stg[hi], ("stg", hi), osb[hi][0:64, :], ("osb", hi), sel, den_ps, rden,
                    o_dst[r0:r0 + 64, c * 512:(c + 1) * 512], use_act=False)))

    n = len(unitsC)
    for i in range(n + 2):
        if i < n:
            C1(i)
        if 1 <= i <= n:
            C2(i - 1)
        if i >= 2:
            C3(i - 2)
    for fn in pending:
        fn()
    pending.clear()


def phase_L2(P, D, l, x_src, blocks, dst, own=False):
    P.barrier()
    P.arena_reset()
    xT = P.a32([128, 8, TB])
    rstd = P.a32([128, 512])
    gs = [P.a32([128, 512]) for _ in range(2)]
    macc = [P.a32([128, 512]) for _ in range(2)]
    mtmp = [P.a32([128, 512]) for _ in range(2)]
    hT = P.a16([128, 8, TB])
    aT = P.a16([128, 22, TB])
    sq = [P.a16([128, 512]) for _ in range(2)]
    oT = P.a16([128, 10, TB])
    mT = P.a16([128, 8, TB])
    wg = [P.a16([128, 3, 8, 128]) for _ in range(2)]
    pw = [P.a16([128, 10, 128]) for _ in range(2)]
    wo = [P.a16([128, 8, 128]) for _ in range(2)]
    wup = [P.a16([128, 2, 8, 128]) for _ in range(2)]
    mflat = mT.rearrange("p k t -> p (k t)")
    wdn = [mflat[:, i * 2816:(i + 1) * 2816].rearrange("p (f c) -> p f c", c=128) for i in range(2)]
    last = {"wo": None, "dn": None}
    ss_ps = P.bank[0][:]
    g_ps = [P.bank[1], P.bank[2]]
    p_ps = [P.bank[3], P.bank[4]]
    y_ps = [P.bank[5], P.bank[6]]
    ones = D.cst[:, 0, :]
    g1 = D.g1s[:, l, :]
    g2 = D.g2s[:, l, :]
    xv = x_src.rearrange("(k p) t -> p k t", p=128)
    ov = (D.o2_scr if own else D.o_scr).rearrange("(j p) t -> p j t", p=128)
    yv = dst.rearrange("(k p) t -> p k t", p=128)
    cnt = {"g": 0, "y": 0, "m": 0}

    for c0 in blocks:
        for tc in range(2):
            ts = slice(tc * 512, (tc + 1) * 512)
            P.op("sp", dma(xT[:, :, ts], xv[:, :, c0 + tc * 512:c0 + (tc + 1) * 512]), writes=[("x", tc)])
        for tc in range(2):
            ts = slice(tc * 512, (tc + 1) * 512)
            P.op("sp", dma(oT[:, :, ts], ov[:, :, c0 + tc * 512:c0 + (tc + 1) * 512]), writes=[("o", tc)])
        emit_xnorm(P, xT, "x", hT, "h", g1, ones, sq, ss_ps, rstd, 2)

        for dc in range(8):
            ws = dc % 2
            P.op("pool", dma(wg[ws], D.wg[l, dc]), writes=[("wg", ws)])
            P.op("pool", dma(pw[ws], D.pw[l, dc]), writes=[("pw", ws)])
            for tc in range(2):
                ts = slice(tc * 512, (tc + 1) * 512)
                ma = cnt["m"] % 2
                cnt["m"] += 1
                for br in range(3):
                    gb = cnt["g"] % 2
                    cnt["g"] += 1
                    for k in range(8):
                        P.op("pe", mm(g_ps[gb][:], wg[ws][:, br, k, :], hT[:, k, ts], k == 0, k == 7),
                             reads=[("wg", ws), ("h", tc)], writes=[("g_ps", gb)])
                    P.op("act", act(gs[gb], g_ps[gb][:], AF.Sigmoid), reads=[("g_ps", gb)], writes=[("gs", gb)])
                    j0, j1 = OCH[br]
                    for j in range(j0, j1):
                        P.op("pe", mm(p_ps[gb][:], pw[ws][:, j, :], oT[:, j, ts], j == j0, j == j1 - 1),
                             reads=[("pw", ws), ("o", tc)], writes=[("p_ps", gb)])
                    if br == 0:
                        P.op("dve", tt(macc[ma], p_ps[gb][:], gs[gb], ALU.mult),
                             reads=[("p_ps", gb), ("gs", gb)], writes=[("macc", ma)])
                    else:
                        P.op("dve", tt(mtmp[ma], p_ps[gb][:], gs[gb], ALU.mult),
                             reads=[("p_ps", gb), ("gs", gb)], writes=[("mtmp", ma)])
                        d2 = macc[ma] if br == 1 else mT[:, dc, ts]
                        dkey = ("macc", ma) if br == 1 else ("m", tc)
                        P.op("dve", tt(d2, macc[ma], mtmp[ma], ALU.add),
                             reads=[("macc", ma), ("mtmp", ma)], writes=[dkey],
                             extra=[last["dn"]] if br == 2 else ())

        for oc in range(8):
            ws = oc % 2
            P.op("pool", dma(wo[ws], D.wo[l, oc]), writes=[("wo", ws)])
            for tc in range(2):
                ts = slice(tc * 512, (tc + 1) * 512)
                yb = cnt["y"] % 2
                cnt["y"] += 1
                for k in range(8):
                    last["wo"] = P.op("pe", mm(y_ps[yb][:], wo[ws][:, k, :], mT[:, k, ts], k == 0, k == 7),
                                      reads=[("wo", ws), ("m", tc)], writes=[("y_ps", yb)])
                P.op("dve", tt(xT[:, oc, ts], y_ps[yb][:], xT[:, oc, ts], ALU.add),
                     reads=[("y_ps", yb), ("x", tc)], writes=[("x", tc)])

        emit_xnorm(P, xT, "x", hT, "h", g2, ones, sq, ss_ps, rstd, 2)
        for fc in range(22):
            ws = fc % 2
            P.op("pool", dma(wup[ws], D.wup[l, fc]), writes=[("wup", ws)])
            for tc in range(2):
                ts = slice(tc * 512, (tc + 1) * 512)
                gb = cnt["g"] % 2
                cnt["g"] += 1
                for k in range(8):
                    P.op("pe", mm(g_ps[gb][:], wup[ws][:, 0, k, :], hT[:, k, ts], k == 0, k == 7),
                         reads=[("wup", ws), ("h", tc)], writes=[("g_ps", gb)])
                for k in range(8):
                    P.op("pe", mm(p_ps[gb][:], wup[ws][:, 1, k, :], hT[:, k, ts], k == 0, k == 7),
                         reads=[("wup", ws), ("h", tc)], writes=[("p_ps", gb)])
                P.op("act", act(gs[gb], g_ps[gb][:], AF.Silu), reads=[("g_ps", gb)], writes=[("gs", gb)])
                P.op("dve", tt(aT[:, fc, ts], p_ps[gb][:], gs[gb], ALU.mult),
                     reads=[("p_ps", gb), ("gs", gb)], writes=[("a", tc)])
        for oc in range(8):
            ws = oc % 2
            P.op("pool", dma(wdn[ws], D.wdn[l, oc]), writes=[("wdn", ws)], extra=[last["wo"]])
            for tc in range(2):
                ts = slice(tc * 512, (tc + 1) * 512)
                yb = cnt["y"] % 2
                cnt["y"] += 1
                for f in range(22):
                    last["dn"] = P.op("pe", mm(y_ps[yb][:], wdn[ws][:, f, :], aT[:, f, ts], f == 0, f == 21),
                                      reads=[("wdn", ws), ("a", tc)], writes=[("y_ps", yb)])
                P.op("dve", tt(xT[:, oc, ts], y_ps[yb][:], xT[:, oc, ts], ALU.add),
                     reads=[("y_ps", yb), ("x", tc)], writes=[("x", tc)])
        for tc in range(2):
            ts = slice(tc * 512, (tc + 1) * 512)
            P.op("sp", dma(yv[:, :, c0 + tc * 512:c0 + (tc + 1) * 512], xT[:, :, ts]), reads=[("x", tc)])


def phase_windows(P, D):
    P.barrier()

    def own(e):
        return P.own(e)
    xs = D.x_scr
    P.op("sp", dmaf(lambda e: D.xo_scr[:, :], lambda e: xs[:, bass.ds(own(e), NT)]))
    P.op("sp", dmaf(lambda e: D.coso[:, :], lambda e: D.cos[:, bass.ds(own(e), NT)]))
    P.op("sp", dmaf(lambda e: D.sino[:, :], lambda e: D.sin[:, bass.ds(own(e), NT)]))
    P.op("sp", dmaf(lambda e: D.kwA[:, :], lambda e: D.qk_scr[6 * 128:12 * 128, bass.ds(own(e), NT + 2 * PADT)]))
    P.op("sp", dmaf(lambda e: D.kwB[:, :],
                    lambda e: D.qk_scr[16 * 128:20 * 128, PADT - 256:][:, bass.ds(own(e), 2560)]))
    P.op("sp", dmaf(lambda e: D.vw[:, :], lambda e: D.v_scr[bass.ds(own(e), NT + 2 * PADT + 16), :]))


def build_program():
    P = Prog()
    D = setup(P)
    phase_init(P, D)
    allblk = [b * 1024 for b in range(SEQ // 1024)]
    ownblk = [b * 1024 for b in range(NT // 1024)]
    phase_L1(P, D, 0, D.xT, allblk)
    for rp in range(4):
        phase_att(P, D, 0, rp, own=False)
    phase_L2(P, D, 0, D.xT, allblk, D.x_scr)
    phase_L1(P, D, 1, D.x_scr, allblk, chunks=K_CHUNKS, do_v=True)
    phase_windows(P, D)
    phase_L1(P, D, 1, D.xo_scr, ownblk, chunks=Q_CHUNKS, do_v=False, own=True)
    for rp in range(4):
        phase_att(P, D, 1, rp, own=True)
    phase_L2(P, D, 1, D.xo_scr, ownblk, D.y, own=True)
    return P.emit()


QK_COLS = ([128 * j for j in range(6)] + [768 + 128 * j for j in range(6)] + [2304 + 128 * j for j in range(4)]
           + [2816 + 128 * j for j in range(4)] + [3840 + 128 * j for j in range(4)] + [4352])
QK_GAIN = [0] * 6 + [1] * 6 + [2] * 4 + [3] * 4 + [4] * 4 + [5]


def chunk_major(w, cols, width):
    K = w.shape[0]
    wk = w.reshape(K // 128, 128, w.shape[1])
    return np.ascontiguousarray(np.stack([wk[:, :, c0:c0 + width].transpose(1, 0, 2) for c0 in cols], 0))


def t5_bucket_np(rel):
    half, max_exact = 16, 8
    ret = np.where(rel > 0, half, 0)
    n = np.abs(rel)
    nf = np.maximum(n, 1).astype(np.float32)
    large = max_exact + (np.log(nf / np.float32(max_exact)) / np.float32(np.log(1024 / max_exact))
                         * np.float32(half - max_exact)).astype(np.int32)
    large = np.minimum(large, half - 1)
    return ret + np.where(n < max_exact, n, large)


def rope_tables():
    t = np.arange(SEQ)
    inv = (np.float32(10000.0) ** (-np.arange(0, 32, 2, dtype=np.float32) / np.float32(32))).astype(np.float32)
    ang_r = (t // 64).astype(np.float32)[:, None] * inv[None, :]
    ang_c = (t % 64).astype(np.float32)[:, None] * inv[None, :]
    cr, sr, cc_, sc_ = np.cos(ang_r), np.sin(ang_r), np.cos(ang_c), np.sin(ang_c)
    cosT = np.concatenate([cr, cr, cc_, cc_], 1).T
    sinT = np.concatenate([-sr, sr, -sc_, sc_], 1).T
    return (np.ascontiguousarray(np.concatenate([cosT, cosT], 0), dtype=np.float32),
            np.ascontiguousarray(np.concatenate([sinT, sinT], 0), dtype=np.float32))


def consts():
    ones = np.ones((128, 128), np.float32)
    blk = np.zeros((128, 128), np.float32)
    blk[:64, :64] = 1
    blk[64:, 64:] = 1
    perm = np.zeros((128, 128), np.float32)
    for m in range(128):
        d = m % 64
        partner = d + 16 if (d % 32) < 16 else d - 16
        perm[(m // 64) * 64 + partner, m] = 1
    return np.stack([ones, blk, perm], 0)


def a_bias(table, r):
    out = np.empty((3, 128, 256), np.float32)
    i = np.arange(128)[:, None]
    j = np.arange(128)[None, :]
    for g, rate in enumerate(A_RATES):
        for t, off in enumerate((-64, 64)):
            rel = i - j + off
            b = table[t5_bucket_np(rel * rate), g * 4 + r]
            out[g, :, t * 128:(t + 1) * 128] = np.where(np.abs(rel) <= 64, b, NEG)
    return out


def b_bias(rpb, r):
    out = np.full((2, NCLS, 128, 640), NEG, np.float32)
    reps = {0: 2, 1: 0, 2: 1, 3: 62, 4: 63}
    ki = np.arange(128)[:, None]
    qj = np.arange(128)[None, :]
    for cls, n in reps.items():
        kts, c2 = b_tiles(n)
        assert c2 == cls
        qrow = 2 * n + qj // 64
        qcol = qj % 64
        rs = np.clip(qrow - 4, 0, 120)
        c0 = np.clip(qcol - 8, 0, 48)
        for idx, kt in enumerate(kts):
            krow = 2 * kt + ki // 64
            kcol = ki % 64
            ok = (krow >= rs) & (krow < rs + 8) & (kcol >= c0) & (kcol < c0 + 16)
            dr = np.clip(krow - qrow + 7, 0, 14)
            dc = np.clip(kcol - qcol + 15, 0, 30)
            for hi in range(2):
                b = rpb[2 * r + hi][dr, dc]
                out[hi, cls, :, idx * 128:(idx + 1) * 128] = np.where(ok, b, NEG)
    return out


def b_bias2(rpb, rp, r):
    out = np.full((2, NCLS, 128, 768), NEG, np.float32)
    reps = {0: 0, 1: 1, 2: 2, 3: 14, 4: 15}
    ki = np.arange(128)[:, None]
    qj = np.arange(128)[None, :]
    for cls, nl in reps.items():
        kts, c2 = b2_tiles(nl)
        assert c2 == cls
        n = 16 * r + nl
        qrow = 2 * n + qj // 64
        qcol = qj % 64
        rs = np.clip(qrow - 4, 0, 120)
        c0 = np.clip(qcol - 8, 0, 48)
        for idx, kt in enumerate(kts):
            krow = 32 * r - 4 + 2 * kt + ki // 64
            kcol = ki % 64
            ok = (krow >= 0) & (krow < 128) & (krow >= rs) & (krow < rs + 8) & (kcol >= c0) & (kcol < c0 + 16)
            dr = np.clip(krow - qrow + 7, 0, 14)
            dc = np.clip(kcol - qcol + 15, 0, 30)
            for hi in range(2):
                b = rpb[2 * rp + hi][dr, dc]
                out[hi, cls, :, idx * 128:(idx + 1) * 128] = np.where(ok, b, NEG)
    return out


def v_aug_tiles(v):
    n = v.shape[0] // 128
    va = np.concatenate([v, np.ones((v.shape[0], 1), v.dtype)], 1).reshape(n, 128, 65)
    return np.ascontiguousarray(va.transpose(1, 0, 2))


def a_v_tiles(v):
    out = np.zeros((3, 128, 80, 65), v[0].dtype)
    for g, rate in enumerate(A_RATES):
        L = SEQ // rate
        va = np.concatenate([v[g], np.ones((SEQ, 1), v[g].dtype)], 1)
        sub = va.reshape(L, rate, 65).transpose(1, 0, 2)
        pad = np.zeros((rate, L + 128, 65), v[g].dtype)
        pad[:, 64:64 + L] = sub
        tiles = pad.reshape(rate * (L // 128 + 1), 128, 65)
        out[g, :, :tiles.shape[0]] = tiles.transpose(1, 0, 2)
    return out


_NC = []


def get_nc():
    if not _NC:
        _NC.append(build_program())
    return _NC[0]


def prep_weights(p):
    m = {}
    wqk, wv, wg, pw, wo, wup, wdn = [], [], [], [], [], [], []
    for l in range(2):
        w_in = p["w_in"][l]
        wqk.append(chunk_major(w_in, QK_COLS, 128))
        vcols = np.concatenate([w_in[:, 1536:2304], w_in[:, 3328:3840], w_in[:, 4480:4608],
                                np.zeros((DM, 128), np.float32)], 1)
        wv.append(chunk_major(vcols, [0, 512, 1024], 512))
        wg.append(np.stack([chunk_major(w_in, [4608 + br * 1024 + dc * 128 for dc in range(8)], 128)
                            for br in range(3)], 2))
        pcat = np.concatenate([p["w_br_a"][l], p["w_br_b"][l], p["w_br_c"][l]], 0)
        pw.append(chunk_major(pcat, [dc * 128 for dc in range(8)], 128))
        wo.append(chunk_major(p["w_o"][l], [oc * 128 for oc in range(8)], 128))
        w_up = p["w_up"][l]
        wup.append(np.stack([chunk_major(w_up, [ab * 2816 + fc * 128 for fc in range(22)], 128)
                             for ab in range(2)], 2))
        wdn.append(chunk_major(p["w_down"][l], [oc * 128 for oc in range(8)], 128))
    m["wqk"] = np.ascontiguousarray(np.stack(wqk, 0))
    m["wv"] = np.ascontiguousarray(np.stack(wv, 0))
    m["wg"] = np.ascontiguousarray(np.stack(wg, 0))
    m["pw"] = np.ascontiguousarray(np.stack(pw, 0))
    m["wo"] = np.ascontiguousarray(np.stack(wo, 0))
    m["wup"] = np.ascontiguousarray(np.stack(wup, 0))
    m["wdn"] = np.ascontiguousarray(np.stack(wdn, 0))
    m["g1c"] = np.ascontiguousarray(p["norm1"].reshape(2, 8, 128).transpose(2, 0, 1))
    m["g2c"] = np.ascontiguousarray(p["norm2"].reshape(2, 8, 128).transpose(2, 0, 1))
    gq = np.stack([np.stack([np.tile(p["qk_gain"][l][gi], 2) for gi in QK_GAIN], 1) for l in range(2)], 1)
    m["gq"] = np.ascontiguousarray(gq)
    cosT, sinT = rope_tables()
    m["cosT"], m["sinT"] = cosT, sinT
    m["cst"] = consts()
    m["abias"] = np.ascontiguousarray(np.stack([a_bias(p["rel_bias_table"], r) for r in range(4)], 0))
    m["bbias"] = np.ascontiguousarray(np.stack(
        [np.stack([b_bias(p["nat_rpb"][l], r) for r in range(4)], 0) for l in range(2)], 0))
    sel = np.zeros((65, 64), np.float32)
    sel[64, :] = 1
    m["sel"] = sel
    return m


def kernel(x, rel_bias_table, norm1, w_in, qk_gain, nat_rpb, w_br_a, w_br_b, w_br_c, w_o, norm2, w_up, w_down):
    p = dict(rel_bias_table=np.asarray(rel_bias_table, np.float32), norm1=np.asarray(norm1, np.float32),
             w_in=np.asarray(w_in, np.float32), qk_gain=np.asarray(qk_gain, np.float32),
             nat_rpb=np.asarray(nat_rpb, np.float32), w_br_a=np.asarray(w_br_a, np.float32),
             w_br_b=np.asarray(w_br_b, np.float32), w_br_c=np.asarray(w_br_c, np.float32),
             w_o=np.asarray(w_o, np.float32), norm2=np.asarray(norm2, np.float32),
             w_up=np.asarray(w_up, np.float32), w_down=np.asarray(w_down, np.float32))
    x = np.asarray(x, np.float32)
    wm = prep_weights(p)
    xTb = [np.ascontiguousarray(x[b].T) for b in range(2)]
    bb2 = [np.ascontiguousarray(np.stack([b_bias2(p["nat_rpb"][1], rp, r) for rp in range(4)], 0)) for r in range(4)]
    in_maps = []
    for c in range(NCORES):
        m = dict(wm)
        m["xT"] = xTb[c // 4]
        m["bbias2"] = bb2[c % 4]
        in_maps.append(m)
    res = run_bass_kernel_spmd(get_nc(), in_maps, core_ids=list(range(NCORES))).results
    out = np.empty((2, SEQ, DM), np.float32)
    for c in range(NCORES):
        out[c // 4, (c % 4) * NT:(c % 4 + 1) * NT, :] = res[c]["yT"].T
    return out
```
